# Optimizing a Trainium2 kernel written in Bass

```python
import math
import jax, jax.numpy as jnp
from jax import lax
import numpy as np

D_MODEL = 4096
BATCH = 4
SEQ = 2048
DEPTH = 4

N_MIXERS = 2
N_A_LAYERS = (DEPTH + N_MIXERS - 1) // N_MIXERS
N_B_LAYERS = DEPTH // N_MIXERS

ROPE_THETA = 10000.0
NORM_EPS = 1e-6
COND_RANK = 512
N_MOD = 6

MLA_HEADS = 32
MLA_Q_RANK = 1536
MLA_KV_RANK = 512
MLA_NOPE_DIM = 128
MLA_ROPE_DIM = 64
MLA_V_DIM = 128
MLA_IN_DIM = MLA_Q_RANK + MLA_KV_RANK + MLA_ROPE_DIM
Q_BLOCK = 128

DIL_GROUPS = ((128, 1), (512, 4), (2048, 16))
N_DIL_GROUPS = len(DIL_GROUPS)
DIL_HEADS = 16
DIL_HEAD_DIM = 128
DIL_QKV_DIM = 3 * N_DIL_GROUPS * DIL_HEADS * DIL_HEAD_DIM
DIL_OUT_DIM = DIL_HEADS * DIL_HEAD_DIM

FFN_HIDDEN = -(-8 * D_MODEL // (3 * 256)) * 256

kernel_name = 'hybrid_mla_dilated_adaln_trunk'


def rms_norm(x, g):
    xf = x.astype(jnp.float32)
    y = xf * lax.rsqrt(jnp.mean(xf * xf, axis=-1, keepdims=True) + NORM_EPS)
    return (y * g.astype(jnp.float32)).astype(x.dtype)


def rope(x, pos):
    d = x.shape[-1]
    half = d // 2
    inv_freq = jnp.exp(jnp.arange(half, dtype=jnp.float32) * (-2.0 * math.log(ROPE_THETA) / d))
    ang = pos.astype(jnp.float32)[:, :, None] * inv_freq
    cos = jnp.cos(ang)[:, :, None, :]
    sin = jnp.sin(ang)[:, :, None, :]
    xf = x.astype(jnp.float32)
    x1, x2 = xf[..., :half], xf[..., half:]
    return jnp.concatenate([x1 * cos - x2 * sin, x2 * cos + x1 * sin], axis=-1).astype(x.dtype)


def mla_mixer(h, pos, w_in, g_q_a, g_kv_a, w_q_b, w_kv_b, g_q_nope, g_q_pe, g_k_nope, g_k_pe, w_o):
    B, S, _ = h.shape
    a = h @ w_in
    cq, ckv, k_pe = jnp.split(a, [MLA_Q_RANK, MLA_Q_RANK + MLA_KV_RANK], axis=-1)
    q = (rms_norm(cq, g_q_a) @ w_q_b).reshape(B, S, MLA_HEADS, MLA_NOPE_DIM + MLA_ROPE_DIM)
    kv = (rms_norm(ckv, g_kv_a) @ w_kv_b).reshape(B, S, MLA_HEADS, MLA_NOPE_DIM + MLA_V_DIM)
    q_nope = rms_norm(q[..., :MLA_NOPE_DIM], g_q_nope)
    q_pe = rope(rms_norm(q[..., MLA_NOPE_DIM:], g_q_pe), pos)
    k_nope = rms_norm(kv[..., :MLA_NOPE_DIM], g_k_nope)
    v = kv[..., MLA_NOPE_DIM:]
    k_pe = rope(rms_norm(k_pe, g_k_pe)[:, :, None, :], pos)[:, :, 0, :]
    scale = (MLA_NOPE_DIM + MLA_ROPE_DIM) ** -0.5
    nb = S // Q_BLOCK

    def to_blocks(t):
        return jnp.moveaxis(t.reshape(B, nb, Q_BLOCK, *t.shape[2:]), 1, 0)

    key_idx = jnp.arange(S, dtype=jnp.int32)

    def attend_block(args):
        qn, qp, start = args
        s = (jnp.einsum('bqhd,bkhd->bhqk', qn, k_nope)
             + jnp.einsum('bqhr,bkr->bhqk', qp, k_pe)).astype(jnp.float32) * scale
        q_idx = start + jnp.arange(Q_BLOCK, dtype=jnp.int32)
        s = jnp.where(key_idx[None, :] <= q_idx[:, None], s, -jnp.inf)
        p = jax.nn.softmax(s, axis=-1)
        return jnp.einsum('bhqk,bkhd->bqhd', p.astype(v.dtype), v)

    starts = jnp.arange(nb, dtype=jnp.int32) * Q_BLOCK
    o = lax.map(attend_block, (to_blocks(q_nope), to_blocks(q_pe), starts))
    o = jnp.moveaxis(o, 0, 1).reshape(B, S, MLA_HEADS * MLA_V_DIM)
    return o @ w_o


def dilated_group_attention(q, k, v, window, dilation):
    B, S, H, d = q.shape
    span = window // dilation
    blk = span
    L = S // dilation
    nb = -(-L // blk)
    Lp = nb * blk

    def to_sub(t):
        t = t.reshape(B, L, dilation, H, d).transpose(0, 2, 1, 3, 4)
        t = jnp.pad(t, ((0, 0), (0, 0), (0, Lp - L), (0, 0), (0, 0)))
        return t.reshape(B, dilation, nb, blk, H, d)

    def band(t):
        prev = jnp.pad(t, ((0, 0), (0, 0), (1, 0), (0, 0), (0, 0), (0, 0)))[:, :, :-1]
        return jnp.concatenate([prev, t], axis=3)

    qb = to_sub(q)
    kb = band(to_sub(k))
    vb = band(to_sub(v))
    s = jnp.einsum('brnqhd,brnkhd->brnhqk', qb, kb).astype(jnp.float32) * (d ** -0.5)
    qi = jnp.arange(blk)[:, None]
    kj = jnp.arange(2 * blk)[None, :]
    dist = blk + qi - kj
    first = (jnp.arange(nb) == 0)[:, None, None]
    valid = (dist >= 0) & (dist <= span) & ~(first & (kj < blk))
    s = jnp.where(valid[None, None, :, None], s, -jnp.inf)
    m = jnp.max(s, axis=-1, keepdims=True)
    e = jnp.exp(s - m)
    den = jnp.sum(e, axis=-1, keepdims=True)
    lse = (m + jnp.log(den))[..., 0]
    o = jnp.einsum('brnhqk,brnkhd->brnqhd', (e / den).astype(v.dtype), vb)

    def from_sub(t):
        t = t.reshape(B, dilation, Lp, *t.shape[4:])[:, :, :L]
        return jnp.swapaxes(t, 1, 2).reshape(B, S, *t.shape[3:])

    return from_sub(o), from_sub(jnp.swapaxes(lse, 3, 4))


def dilated_mixer(h, pos, w_qkv, g_q, g_k, w_o):
    B, S, _ = h.shape
    G, HG, dh = N_DIL_GROUPS, DIL_HEADS, DIL_HEAD_DIM
    qkv = (h @ w_qkv).reshape(B, S, 3, G, HG, dh)
    q = rope(rms_norm(qkv[:, :, 0], g_q[:, None, :]).reshape(B, S, G * HG, dh), pos)
    k = rope(rms_norm(qkv[:, :, 1], g_k[:, None, :]).reshape(B, S, G * HG, dh), pos)
    q = q.reshape(B, S, G, HG, dh)
    k = k.reshape(B, S, G, HG, dh)
    v = qkv[:, :, 2]
    outs, lses = [], []
    for gi, (window, dilation) in enumerate(DIL_GROUPS):
        o_g, l_g = dilated_group_attention(q[:, :, gi], k[:, :, gi], v[:, :, gi], window, dilation)
        outs.append(o_g)
        lses.append(l_g)
    alpha = jax.nn.softmax(jnp.stack(lses).astype(jnp.float32), axis=0)
    o = jnp.sum(alpha[..., None] * jnp.stack(outs).astype(jnp.float32), axis=0).astype(h.dtype)
    return o.reshape(B, S, DIL_OUT_DIM) @ w_o


def swiglu(h, w_gate, w_up, w_down):
    return (jax.nn.silu(h @ w_gate) * (h @ w_up)) @ w_down


def setup_inputs(seed: int = 0) -> dict:
    key = jax.random.key(seed)
    ks = jax.random.split(key, 26)
    D = D_MODEL

    def nrm(k, shape, scale):
        return jax.random.normal(k, shape, jnp.float32) * scale

    def gain(k, shape):
        return 1.0 + 0.02 * jax.random.normal(k, shape, jnp.float32)

    offset = jax.random.randint(ks[2], (BATCH,), 0, 1024, dtype=jnp.int32)
    positions = (offset[:, None] + jnp.arange(SEQ, dtype=jnp.int32)[None, :]).astype(jnp.int32)
    qk_dim = MLA_NOPE_DIM + MLA_ROPE_DIM
    return {
        'x': nrm(ks[0], (BATCH, SEQ, D), 1.0),
        'c': nrm(ks[1], (BATCH, D), 1.0),
        'positions': positions,
        'w_cond': nrm(ks[3], (D, COND_RANK), D ** -0.5),
        'b_cond': nrm(ks[4], (COND_RANK,), 0.02),
        'w_mod': nrm(ks[5], (DEPTH, COND_RANK, N_MOD * D), 0.5 * COND_RANK ** -0.5),
        'b_mod': nrm(ks[6], (DEPTH, N_MOD * D), 0.02),
        'g_mix_norm': gain(ks[7], (DEPTH, D)),
        'g_ffn_norm': gain(ks[8], (DEPTH, D)),
        'mla_w_in': nrm(ks[9], (N_A_LAYERS, D, MLA_IN_DIM), D ** -0.5),
        'mla_g_q_a': gain(ks[10], (N_A_LAYERS, MLA_Q_RANK)),
        'mla_g_kv_a': gain(ks[11], (N_A_LAYERS, MLA_KV_RANK)),
        'mla_w_q_b': nrm(ks[12], (N_A_LAYERS, MLA_Q_RANK, MLA_HEADS * qk_dim), MLA_Q_RANK ** -0.5),
        'mla_w_kv_b': nrm(ks[13], (N_A_LAYERS, MLA_KV_RANK, MLA_HEADS * (MLA_NOPE_DIM + MLA_V_DIM)), MLA_KV_RANK ** -0.5),
        'mla_g_q_nope': gain(ks[14], (N_A_LAYERS, MLA_NOPE_DIM)),
        'mla_g_q_pe': gain(ks[15], (N_A_LAYERS, MLA_ROPE_DIM)),
        'mla_g_k_nope': gain(ks[16], (N_A_LAYERS, MLA_NOPE_DIM)),
        'mla_g_k_pe': gain(ks[17], (N_A_LAYERS, MLA_ROPE_DIM)),
        'mla_w_o': nrm(ks[18], (N_A_LAYERS, MLA_HEADS * MLA_V_DIM, D), (MLA_HEADS * MLA_V_DIM) ** -0.5),
        'dil_w_qkv': nrm(ks[19], (N_B_LAYERS, D, DIL_QKV_DIM), D ** -0.5),
        'dil_g_q': gain(ks[20], (N_B_LAYERS, N_DIL_GROUPS, DIL_HEAD_DIM)),
        'dil_g_k': gain(ks[21], (N_B_LAYERS, N_DIL_GROUPS, DIL_HEAD_DIM)),
        'dil_w_o': nrm(ks[22], (N_B_LAYERS, DIL_OUT_DIM, D), DIL_OUT_DIM ** -0.5),
        'ffn_w_gate': nrm(ks[23], (DEPTH, D, FFN_HIDDEN), D ** -0.5),
        'ffn_w_up': nrm(ks[24], (DEPTH, D, FFN_HIDDEN), D ** -0.5),
        'ffn_w_down': nrm(ks[25], (DEPTH, FFN_HIDDEN, D), FFN_HIDDEN ** -0.5),
    }


def reference(x, c, positions, w_cond, b_cond, w_mod, b_mod, g_mix_norm, g_ffn_norm,
              mla_w_in, mla_g_q_a, mla_g_kv_a, mla_w_q_b, mla_w_kv_b, mla_g_q_nope,
              mla_g_q_pe, mla_g_k_nope, mla_g_k_pe, mla_w_o, dil_w_qkv, dil_g_q, dil_g_k,
              dil_w_o, ffn_w_gate, ffn_w_up, ffn_w_down):
    e = jax.nn.silu(c @ w_cond + b_cond)
    for i in range(DEPTH):
        mod = (e @ w_mod[i] + b_mod[i])[:, None, :]
        sh_m, sc_m, gt_m, sh_f, sc_f, gt_f = jnp.split(mod, N_MOD, axis=-1)
        h = rms_norm(x, g_mix_norm[i]) * (1.0 + sc_m) + sh_m
        j = i // N_MIXERS
        if i % N_MIXERS == 0:
            y = mla_mixer(h, positions, mla_w_in[j], mla_g_q_a[j], mla_g_kv_a[j], mla_w_q_b[j],
                          mla_w_kv_b[j], mla_g_q_nope[j], mla_g_q_pe[j], mla_g_k_nope[j],
                          mla_g_k_pe[j], mla_w_o[j])
        else:
            y = dilated_mixer(h, positions, dil_w_qkv[j], dil_g_q[j], dil_g_k[j], dil_w_o[j])
        x = x + gt_m * y
        h = rms_norm(x, g_ffn_norm[i]) * (1.0 + sc_f) + sh_f
        x = x + gt_f * swiglu(h, ffn_w_gate[i], ffn_w_up[i], ffn_w_down[i])
    return x
```

```python
import math
import numpy as np
from contextlib import ExitStack
import concourse.bass as bass
import concourse.mybir as mybir
from concourse.bass_utils import run_bass_kernel_spmd

F32 = mybir.dt.float32
BF16 = mybir.dt.bfloat16
I32 = mybir.dt.int32
AF = mybir.ActivationFunctionType
ALU = mybir.AluOpType

D = 4096
SEQ = 2048
TH = 1024
DC = D // 128
HID = 11008
DEPTH = 4
EPS = 1e-6
ARENA_WORDS = 49152
N_ACTIVE = 4
import os
MKX = os.environ.get('MKX', '')


class Op:
    __slots__ = ("eng", "fn", "deps", "signal", "ev", "dma_key", "ndma", "idx", "inc")

    def __init__(self, eng, fn, dma_key=None, ndma=1, inc=16):
        self.eng = eng
        self.fn = fn
        self.deps = []
        self.signal = False
        self.ev = None
        self.dma_key = dma_key
        self.ndma = ndma
        self.inc = inc


class Sched:
    def __init__(self):
        self.ops = []
        self.last_write = {}
        self.readers = {}
        self.last_on = {}
        self.pending_dma = []

    def add(self, eng, fn, reads=(), writes=(), dma_key=None, ndma=1, inc=16):
        op = Op(eng, fn, dma_key, ndma, inc)
        op.idx = len(self.ops)
        deps = set()
        for r in reads:
            w = self.last_write.get(r)
            if w is not None:
                deps.add(w)
        for wkey in writes:
            w = self.last_write.get(wkey)
            if w is not None:
                deps.add(w)
            for rd in self.readers.get(wkey, ()):
                deps.add(rd)
        op.deps = sorted(deps)
        for wkey in writes:
            self.last_write[wkey] = op.idx
            self.readers[wkey] = []
        for r in reads:
            if r in writes:
                continue
            self.readers.setdefault(r, []).append(op.idx)
        self.ops.append(op)
        self.last_on[eng] = op.idx
        if dma_key is not None:
            self.pending_dma.append(op.idx)
        return op

    def barrier(self, engines=("pe", "act", "dve", "pool", "sp")):
        deps = sorted(set(self.last_on.values()) | set(self.pending_dma))
        self.pending_dma = []
        self.last_write = {}
        self.readers = {}
        for eng in engines:
            op = Op(eng, None)
            op.idx = len(self.ops)
            op.deps = list(deps)
            self.ops.append(op)
            self.last_on[eng] = op.idx

    def emit(self, nc, stack):
        ops = self.ops
        for op in ops:
            for d in op.deps:
                ops[d].signal = True
            if op.dma_key is not None:
                op.signal = True
        cnt = {}
        order = []
        for op in ops:
            if not op.signal or op.fn is None:
                continue
            key = ("dma", op.dma_key) if op.dma_key is not None else ("eng", op.eng)
            if key not in cnt:
                cnt[key] = 0
                order.append(key)
            cnt[key] += op.inc * op.ndma if op.dma_key is not None else 1
            op.ev = (key, cnt[key])
        sems = {}
        for key in order:
            sems[key] = stack.enter_context(nc.semaphore("s_%s_%s" % key))
        self.nsems = len(sems)
        streams = {}
        for op in ops:
            streams.setdefault(op.eng, []).append(op)
        block = stack.enter_context(nc.Block())

        def run(e, engname):
            seen = {}
            for op in streams.get(engname, []):
                need = {}
                for d in op.deps:
                    p = ops[d]
                    if p.ev is None:
                        continue
                    k, c = p.ev
                    if need.get(k, 0) < c:
                        need[k] = c
                for k, c in need.items():
                    if seen.get(k, 0) < c:
                        e.wait_ge(sems[k], c)
                        seen[k] = c
                if op.fn is None:
                    continue
                if op.dma_key is not None:
                    op.fn(e, sems[("dma", op.dma_key)])
                else:
                    ins = op.fn(e)
                    if op.signal:
                        ins.then_inc(sems[("eng", op.eng)], 1)

        if "pe" in streams:
            @block.tensor
            def _(e):
                run(e, "pe")
        if "act" in streams:
            @block.scalar
            def _(e):
                run(e, "act")
        if "dve" in streams:
            @block.vector
            def _(e):
                run(e, "dve")
        if "pool" in streams:
            @block.gpsimd
            def _(e):
                run(e, "pool")
        if "sp" in streams:
            @block.sync
            def _(e):
                run(e, "sp")


class Ctx:
    def __init__(self, nc, st):
        self.nc = nc
        self.st = st
        self.S = Sched()
        self.arena = st.enter_context(nc.sbuf_tensor("arena", [128, ARENA_WORDS], F32))
        self.ps = [st.enter_context(nc.psum_tensor("ps%d" % i, [128, 512], F32)) for i in range(8)]
        self.ps_free = list(range(8))
        self.off = 0
        self.rot = {}
        self.aux = []

    def mark(self):
        return self.off

    def reset(self, off):
        self.off = off

    def alloc(self, dtype, *free):
        esz = 4 if dtype in (F32, I32) else 2
        n = 1
        for f in free:
            n *= f
        nbytes = (n * esz + 31) // 32 * 32
        off = self.off
        self.off += nbytes
        assert self.off <= ARENA_WORDS * 4, "arena overflow %d" % self.off
        v = self.arena[:, off // 4:(off + nbytes) // 4]
        if dtype != F32:
            v = v.bitcast(dtype)
        v = v[:, 0:n]
        if len(free) == 2:
            v = v.rearrange("p (a b) -> p a b", b=free[1])
        elif len(free) == 3:
            v = v.rearrange("p (a b c) -> p a b c", b=free[1], c=free[2])
        return v

    def psum(self):
        i = self.ps_free.pop(0)
        self.ps_free.append(i)
        return i, self.ps[i]

    def set_aux(self, n):
        self.aux = [self.ps_free.pop(0) for _ in range(n)]

    def clear_aux(self):
        self.ps_free.extend(self.aux)
        self.aux = []

    def psum_aux(self):
        i = self.aux.pop(0)
        self.aux.append(i)
        return i, self.ps[i]

    def reserve(self, n):
        out = []
        for _ in range(n):
            i = self.ps_free.pop(0)
            out.append((i, self.ps[i]))
        return out

    def release(self, banks):
        for i, _ in banks:
            self.ps_free.append(i)

    def nxt(self, name, n):
        v = self.rot.get(name, 0)
        self.rot[name] = v + 1
        return v % n


def dma(K, eng, key, out, in_, reads=(), writes=()):
    def fn(e, sem):
        e.dma_start(out=out, in_=in_).then_inc(sem, 16)
    K.S.add(eng, fn, reads=reads, writes=writes, dma_key=key, ndma=1)


def dma_multi(K, eng, key, pairs, reads=(), writes=()):
    def fn(e, sem):
        for (o, i) in pairs:
            e.dma_start(out=o, in_=i).then_inc(sem, 16)
    K.S.add(eng, fn, reads=reads, writes=writes, dma_key=key, ndma=len(pairs))


def PS(b):
    return ("ps", b[0])


V_C = 0
V_BCOND = 32
V_LAYER = 36
V_MLA = V_LAYER + 4 * 256
V_DIL = V_MLA + 40
V_CONST = V_DIL + 12
NV = V_CONST + 4
C_PERM_D = 0
C_PERM_M = 128
C_BD = 256
C_TRI = 384
C_DMASK = 512
NCM = 768


def setup(K, vecs_d, cmat_d):
    S = K.S
    K.vec = K.alloc(F32, NV)
    dma(K, "sp", "vec", K.vec, vecs_d, writes=["vec"])
    K.ones_f = K.alloc(F32, 128)
    K.ones_b = K.alloc(BF16, 128)
    K.cm = K.alloc(BF16, NCM)
    K.modv = K.alloc(F32, 6 * DC)
    K.e_b = K.alloc(BF16, 4)
    S.add("dve", lambda e: e.memset(K.ones_f, 1.0), writes=["ones_f"])
    S.add("dve", lambda e: e.memset(K.ones_b, 1.0), writes=["ones_b"])
    m0 = K.mark()
    cmf = K.alloc(F32, NCM)
    dma(K, "sp", "cmf", cmf, cmat_d, writes=["cmf"])
    S.add("dve", lambda e: e.tensor_copy(out=K.cm, in_=cmf), reads=["cmf"], writes=["cm"])
    S.barrier()
    K.reset(m0)
    K.perm_d = K.cm[:, C_PERM_D:C_PERM_D + 128]
    K.perm_m = K.cm[:, C_PERM_M:C_PERM_M + 128]
    K.bd = K.cm[:, C_BD:C_BD + 128]
    K.tri = K.cm[:, C_TRI:C_TRI + 128]
    K.dmask = K.cm[:, C_DMASK:C_DMASK + 256]


def rope_tables(K, posb_d, tabs_d):
    S = K.S
    m0 = K.mark()
    pi = K.alloc(I32, SEQ)
    pf = K.alloc(F32, SEQ)
    ang = K.alloc(F32, SEQ)
    z = K.alloc(F32, SEQ)
    a2 = K.alloc(F32, SEQ)
    o = K.alloc(F32, SEQ)
    dma(K, "sp", "pos", pi, posb_d, writes=["pi"])
    S.add("dve", lambda e: e.tensor_copy(out=pf, in_=pi), reads=["pi"], writes=["pf"])
    TWO_PI = 2.0 * math.pi
    for kind in range(2):
        invf = K.vec[:, V_CONST + kind:V_CONST + kind + 1]
        sgn = K.vec[:, V_CONST + 2 + kind:V_CONST + 3 + kind]
        S.add("dve", lambda e, invf=invf: e.tensor_scalar(out=ang, in0=pf, scalar1=invf, scalar2=None, op0=ALU.mult),
              reads=["pf", "vec"], writes=["ang"])
        for cs in range(2):
            shift = 0.5 * math.pi if cs == 0 else 0.0
            S.add("dve", lambda e, shift=shift: e.tensor_scalar(out=a2, in0=ang, scalar1=shift, scalar2=None, op0=ALU.add),
                  reads=["ang"], writes=["a2"])
            S.add("dve", lambda e: e.tensor_scalar(out=z, in0=a2, scalar1=1.0 / TWO_PI, scalar2=None, op0=ALU.mult),
                  reads=["a2"], writes=["z"])
            S.add("dve", lambda e: e.tensor_copy(out=pi, in_=z), reads=["z", "pf"], writes=["pi"])
            S.add("dve", lambda e: e.tensor_copy(out=z, in_=pi), reads=["pi"], writes=["z"])
            S.add("dve", lambda e: e.scalar_tensor_tensor(out=a2, in0=z, scalar=-TWO_PI, in1=a2, op0=ALU.mult, op1=ALU.add),
                  reads=["z", "a2"], writes=["a2"])
            S.add("dve", lambda e: e.tensor_scalar(out=z, in0=a2, scalar1=math.pi, scalar2=TWO_PI, op0=ALU.is_gt, op1=ALU.mult),
                  reads=["a2"], writes=["z"])
            S.add("dve", lambda e: e.tensor_tensor(out=z, in0=a2, in1=z, op=ALU.subtract), reads=["a2", "z"], writes=["z"])
            if cs == 0:
                S.add("act", lambda e: e.activation(out=o, in_=z, func=AF.Sin), reads=["z"], writes=["o"])
            else:
                S.add("act", lambda e, sgn=sgn: e.activation(out=z, in_=z, func=AF.Sin), reads=["z"], writes=["z"])
                S.add("dve", lambda e, sgn=sgn: e.tensor_scalar(out=o, in0=z, scalar1=sgn, scalar2=None, op0=ALU.mult),
                      reads=["z", "vec"], writes=["o"])
            dma(K, "sp", "tabo", tabs_d[2 * kind + cs], o, reads=["o"], writes=[("tab", 2 * kind + cs)])
    S.barrier()
    K.reset(m0)


def cond_embed(K, w_cond_d):
    S = K.S
    m0 = K.mark()
    wt = K.alloc(BF16, DC, 512)
    cb = K.alloc(BF16, DC)
    ef = K.alloc(F32, 4)
    dma(K, "pool", "wcond", wt, w_cond_d.rearrange("(c p) n -> p c n", p=128), writes=["wt"])
    S.add("dve", lambda e: e.tensor_copy(out=cb, in_=K.vec[:, V_C:V_C + DC]), reads=["vec"], writes=["cb"])
    b = K.psum()

    def mm(e):
        ins = None
        for rc in range(4):
            for k in range(DC):
                ins = e.matmul(b[1][:, rc:rc + 1], lhsT=wt[:, k, rc * 128:(rc + 1) * 128], rhs=cb[:, k:k + 1],
                               start=(k == 0), stop=(k == DC - 1))
        return ins
    S.add("pe", mm, reads=["wt", "cb"], writes=[PS(b)])
    S.add("dve", lambda e: e.tensor_tensor(out=ef, in0=b[1][:, 0:4], in1=K.vec[:, V_BCOND:V_BCOND + 4], op=ALU.add),
          reads=[PS(b), "vec"], writes=["ef"])
    S.add("act", lambda e: e.activation(out=K.e_b, in_=ef, func=AF.Silu), reads=["ef"], writes=["e_b"])
    S.barrier()
    K.reset(m0)


def mod_layer(K, w_mod_d, li):
    S = K.S
    m0 = K.mark()
    wts = [K.alloc(BF16, 4, 2048) for _ in range(2)]
    mod = K.alloc(F32, 6 * DC)
    wv = w_mod_d.rearrange("(c p) n -> p c n", p=128)
    b = K.reserve(1)[0]
    for t in range(12):
        s = t % 2
        dma(K, "pool", "wmod%d" % s, wts[s], wv[:, :, t * 2048:(t + 1) * 2048], writes=[("wm", s)])

        def mm(e, s=s, t=t):
            ins = None
            for jj in range(16):
                j = t * 16 + jj
                for rc in range(4):
                    ins = e.matmul(b[1][:, j:j + 1], lhsT=wts[s][:, rc, jj * 128:(jj + 1) * 128],
                                   rhs=K.e_b[:, rc:rc + 1], start=(rc == 0), stop=(rc == 3))
            return ins
        S.add("pe", mm, reads=[("wm", s), "e_b"], writes=[PS(b)])
    vb = V_LAYER + li * 256
    S.add("dve", lambda e: e.tensor_tensor(out=mod, in0=b[1][:, 0:6 * DC], in1=K.vec[:, vb:vb + 192], op=ALU.add),
          reads=[PS(b), "vec"], writes=["mod"])
    mv = K.modv

    def fin(e):
        e.scalar_tensor_tensor(out=mv[:, 0:DC], in0=mod[:, DC:2 * DC], scalar=1.0, in1=K.vec[:, vb + 192:vb + 224],
                               op0=ALU.add, op1=ALU.mult)
        e.tensor_copy(out=mv[:, DC:2 * DC], in_=mod[:, 0:DC])
        e.tensor_copy(out=mv[:, 2 * DC:3 * DC], in_=mod[:, 2 * DC:3 * DC])
        e.scalar_tensor_tensor(out=mv[:, 3 * DC:4 * DC], in0=mod[:, 4 * DC:5 * DC], scalar=1.0,
                               in1=K.vec[:, vb + 224:vb + 256], op0=ALU.add, op1=ALU.mult)
        e.tensor_copy(out=mv[:, 4 * DC:5 * DC], in_=mod[:, 3 * DC:4 * DC])
        return e.tensor_copy(out=mv[:, 5 * DC:6 * DC], in_=mod[:, 5 * DC:6 * DC])
    S.add("dve", fin, reads=["mod", "vec"], writes=["modv"])
    K.release([b])
    S.barrier()
    K.reset(m0)


def norm_phase(K, x_d, t0, A, B, hT):
    S = K.S
    T = TH
    NH = T // 512
    m0 = K.mark()
    xs = [K.alloc(F32, T) for _ in range(2)]
    sq = [K.alloc(F32, T) for _ in range(2)]
    rstd = K.alloc(F32, T)
    pst = K.reserve(NH)
    xv = x_d.rearrange("(c p) t -> c p t", p=128)
    for c in range(DC):
        s = c % 2
        dma(K, "sp", "nxs%d" % s, xs[s], xv[c][:, t0:t0 + T], writes=[("xs", s)])
        S.add("act", lambda e, s=s: e.activation(out=sq[s], in_=xs[s], func=AF.Square),
              reads=[("xs", s)], writes=[("sq", s)])

        def mm(e, s=s, c=c):
            ins = None
            for h in range(NH):
                ins = e.matmul(pst[h][1][:, :], lhsT=K.ones_f, rhs=sq[s][:, h * 512:(h + 1) * 512],
                               start=(c == 0), stop=(c == DC - 1))
            return ins
        S.add("pe", mm, reads=[("sq", s)], writes=[PS(p) for p in pst])
    for h in range(NH):
        S.add("act", lambda e, h=h: e.activation(out=rstd[:, h * 512:(h + 1) * 512], in_=pst[h][1][:, :],
                                                   func=AF.Sqrt, scale=1.0 / D, bias=EPS),
              reads=[PS(pst[h])], writes=[("rs", h)])
        S.add("dve", lambda e, h=h: e.reciprocal(out=rstd[:, h * 512:(h + 1) * 512], in_=rstd[:, h * 512:(h + 1) * 512]),
              reads=[("rs", h)], writes=[("rs", h)])
    for c in range(DC):
        s = c % 2
        dma(K, "sp", "nxs%d" % s, xs[s], xv[c][:, t0:t0 + T], writes=[("xs", s)])
        S.add("dve", lambda e, s=s: e.tensor_tensor(out=sq[s], in0=xs[s], in1=rstd, op=ALU.mult),
              reads=[("xs", s)] + [("rs", h) for h in range(NH)], writes=[("sq", s)])
        S.add("act", lambda e, s=s, c=c: e.activation(out=hT[:, c, :], in_=sq[s], func=AF.Identity,
                                                        scale=A[:, c:c + 1], bias=B[:, c:c + 1]),
              reads=[("sq", s), "modv"], writes=[("hT", c)])
    K.release(pst)
    S.barrier()
    K.reset(m0)


def proj_residual(K, hT, KC, W_d, gt, x_d, t0, tag):
    S = K.S
    T = TH
    NH = T // 512
    m0 = K.mark()
    wd = [K.alloc(BF16, KC, 256) for _ in range(2)]
    xo = [K.alloc(F32, T) for _ in range(2)]
    Wv = W_d.rearrange("(c p) n -> p c n", p=128)
    xv = x_d.rearrange("(c p) t -> c p t", p=128)
    for n2 in range(DC // 2):
        s = n2 % 2
        dma(K, "pool", tag + "w%d" % s, wd[s], Wv[:, :, n2 * 256:(n2 + 1) * 256], writes=[("wd", s)])
        for nn in range(2):
            n = 2 * n2 + nn
            po = [K.psum() for _ in range(NH)]

            def mm2(e, s=s, nn=nn, po=po):
                ins = None
                for k in range(KC):
                    for h in range(NH):
                        ins = e.matmul(po[h][1][:, :], lhsT=wd[s][:, k, nn * 128:(nn + 1) * 128],
                                       rhs=hT[:, k, h * 512:(h + 1) * 512], start=(k == 0), stop=(k == KC - 1))
                return ins
            S.add("pe", mm2, reads=[("wd", s), "hT"], writes=[PS(p) for p in po])
            q = n % 2
            dma(K, "sp", tag + "xi%d" % q, xo[q], xv[n][:, t0:t0 + T], writes=[("xo", q)])
            for h in range(NH):
                S.add("dve", lambda e, q=q, h=h, po=po, n=n: e.scalar_tensor_tensor(
                    out=xo[q][:, h * 512:(h + 1) * 512], in0=po[h][1][:, :], scalar=gt[:, n:n + 1],
                    in1=xo[q][:, h * 512:(h + 1) * 512], op0=ALU.mult, op1=ALU.add),
                    reads=[PS(po[h]), ("xo", q), "modv"], writes=[("xo", q)])
            dma(K, "sp", tag + "xo%d" % q, xv[n][:, t0:t0 + T], xo[q], reads=[("xo", q)])
    S.barrier()
    K.reset(m0)


def ffn_phase(K, hT, x_d, t0, Wg, Wu, Wd, gt, GJ=22):
    S = K.S
    T = TH
    H = HID
    HC = H // 128
    NH = T // 512
    m0 = K.mark()
    wg = [K.alloc(BF16, DC, 128) for _ in range(2)]
    wu = [K.alloc(BF16, DC, 128) for _ in range(2)]
    act = K.alloc(BF16, GJ, T)
    wd = [K.alloc(BF16, GJ, 256) for _ in range(2)]
    sl = [K.alloc(F32, 512) for _ in range(2)]
    xo = [K.alloc(F32, T) for _ in range(2)]
    Wgv = Wg.rearrange("(c p) n -> p c n", p=128)
    Wuv = Wu.rearrange("(c p) n -> p c n", p=128)
    Wdv = Wd.rearrange("(c p) n -> p c n", p=128)
    xv = x_d.rearrange("(c p) t -> c p t", p=128)
    groups = [(j0, min(j0 + GJ, HC)) for j0 in range(0, HC, GJ)]
    for (j0, j1) in groups:
        nj = j1 - j0
        for j in range(j0, j1):
            s = K.nxt("fj", 2)
            dma(K, "pool", "fwg%d" % s, wg[s], Wgv[:, :, j * 128:(j + 1) * 128], writes=[("wg", s)])
            dma(K, "pool", "fwu%d" % s, wu[s], Wuv[:, :, j * 128:(j + 1) * 128], writes=[("wu", s)])
            pg = [K.psum() for _ in range(NH)]
            pu = [K.psum() for _ in range(NH)]

            def mm(e, s=s, pg=pg, pu=pu):
                ins = None
                for k in range(DC):
                    for h in range(NH):
                        ins = e.matmul(pg[h][1][:, :], lhsT=wg[s][:, k, :], rhs=hT[:, k, h * 512:(h + 1) * 512],
                                       start=(k == 0), stop=(k == DC - 1))
                for k in range(DC):
                    for h in range(NH):
                        ins = e.matmul(pu[h][1][:, :], lhsT=wu[s][:, k, :], rhs=hT[:, k, h * 512:(h + 1) * 512],
                                       start=(k == 0), stop=(k == DC - 1))
                return ins
            S.add("pe", mm, reads=[("wg", s), ("wu", s), "hT"], writes=[PS(p) for p in pg + pu])
            for h in range(NH):
                q = K.nxt("fsl", 2)
                S.add("act", lambda e, q=q, h=h, pg=pg: e.activation(out=sl[q], in_=pg[h][1][:, :], func=AF.Silu),
                      reads=[PS(pg[h])], writes=[("sl", q)])
                S.add("dve", lambda e, q=q, h=h, pu=pu, jj=j - j0: e.tensor_tensor(
                    out=act[:, jj, h * 512:(h + 1) * 512], in0=sl[q], in1=pu[h][1][:, :], op=ALU.mult),
                    reads=[("sl", q), PS(pu[h])], writes=[("act", j - j0, h)])
        for n2 in range(DC // 2):
            s = K.nxt("fn", 2)
            dma(K, "pool", "fwd%d" % s, wd[s][:, 0:nj, :], Wdv[:, j0:j1, n2 * 256:(n2 + 1) * 256], writes=[("wd", s)])
            for nn in range(2):
                n = 2 * n2 + nn
                po = [K.psum() for _ in range(NH)]

                def mm2(e, s=s, nn=nn, po=po, nj=nj):
                    ins = None
                    for jj in range(nj):
                        for h in range(NH):
                            ins = e.matmul(po[h][1][:, :], lhsT=wd[s][:, jj, nn * 128:(nn + 1) * 128],
                                           rhs=act[:, jj, h * 512:(h + 1) * 512],
                                           start=(jj == 0), stop=(jj == nj - 1))
                    return ins
                S.add("pe", mm2, reads=[("wd", s)] + [("act", jj, h) for jj in range(nj) for h in range(NH)],
                      writes=[PS(p) for p in po])
                q = K.nxt("fxo", 2)
                dma(K, "sp", "fxi%d" % q, xo[q], xv[n][:, t0:t0 + T], reads=[("xd", n)], writes=[("xo", q)])
                for h in range(NH):
                    S.add("dve", lambda e, q=q, h=h, po=po, n=n: e.scalar_tensor_tensor(
                        out=xo[q][:, h * 512:(h + 1) * 512], in0=po[h][1][:, :], scalar=gt[:, n:n + 1],
                        in1=xo[q][:, h * 512:(h + 1) * 512], op0=ALU.mult, op1=ALU.add),
                        reads=[PS(po[h]), ("xo", q), "modv"], writes=[("xo", q)])
                dma(K, "sp", "fxo%d" % q, xv[n][:, t0:t0 + T], xo[q], reads=[("xo", q)], writes=[("xd", n)])
    S.barrier()
    K.reset(m0)


class HN:
    def __init__(self, K):
        self.K = K
        self.sq = [K.alloc(BF16, 512) for _ in range(2)]
        self.rs = [K.alloc(F32, 512) for _ in range(2)]
        self.xb = [K.alloc(BF16, 512) for _ in range(2)]
        self.t1 = [K.alloc(F32, 512) for _ in range(2)]
        self.t2 = [K.alloc(F32, 512) for _ in range(2)]

    def run(self, pb, dim, gain, ones_mat, out_ap, out_key, pre=None, rope=None, P=128):
        K = self.K
        S = K.S
        s = K.nxt("hn", 2)
        sq, rs, xb, t1, t2 = (self.sq[s][0:P], self.rs[s][0:P], self.xb[s][0:P], self.t1[s][0:P], self.t2[s][0:P])
        pbv = pb[1][0:P, :]
        S.add("act", lambda e: e.activation(out=sq, in_=pbv, func=AF.Square), reads=[PS(pb)], writes=[("hsq", s)])
        ss = K.psum_aux()
        S.add("pe", lambda e: e.matmul(ss[1][0:P, :], lhsT=ones_mat, rhs=sq, start=True, stop=True),
              reads=[("hsq", s)], writes=[PS(ss)])
        if pre is not None:
            pre_r, pre_r2, pkey = pre
            S.add("dve", lambda e: e.tensor_tensor(out=rs, in0=ss[1][0:P, :], in1=pre_r2, op=ALU.mult),
                  reads=[PS(ss), pkey], writes=[("hrs", s)])
            S.add("act", lambda e: e.activation(out=rs, in_=rs, func=AF.Sqrt, scale=1.0 / dim, bias=EPS),
                  reads=[("hrs", s)], writes=[("hrs", s)])
        else:
            S.add("act", lambda e: e.activation(out=rs, in_=ss[1][0:P, :], func=AF.Sqrt, scale=1.0 / dim, bias=EPS),
                  reads=[PS(ss)], writes=[("hrs", s)])
        S.add("dve", lambda e: e.reciprocal(out=rs, in_=rs), reads=[("hrs", s)], writes=[("hrs", s)])
        if pre is not None:
            S.add("dve", lambda e: e.tensor_tensor(out=rs, in0=rs, in1=pre_r, op=ALU.mult),
                  reads=[("hrs", s), pkey], writes=[("hrs", s)])
        if rope is None:
            S.add("dve", lambda e: e.scalar_tensor_tensor(out=out_ap, in0=pbv, scalar=gain, in1=rs,
                                                          op0=ALU.mult, op1=ALU.mult),
                  reads=[PS(pb), ("hrs", s), "vec"], writes=[out_key])
        else:
            cos, sinS, perm, rkey = rope
            S.add("dve", lambda e: e.scalar_tensor_tensor(out=xb, in0=pbv, scalar=gain, in1=rs,
                                                          op0=ALU.mult, op1=ALU.mult),
                  reads=[PS(pb), ("hrs", s), "vec"], writes=[("hxb", s)])
            rot = K.psum_aux()
            S.add("pe", lambda e: e.matmul(rot[1][0:P, :], lhsT=perm, rhs=xb, start=True, stop=True),
                  reads=[("hxb", s)], writes=[PS(rot)])
            S.add("dve", lambda e: e.tensor_tensor(out=t1, in0=xb, in1=cos, op=ALU.mult),
                  reads=[("hxb", s), rkey], writes=[("ht1", s)])
            S.add("dve", lambda e: e.tensor_tensor(out=t2, in0=rot[1][0:P, :], in1=sinS, op=ALU.mult),
                  reads=[PS(rot), rkey], writes=[("ht2", s)])
            S.add("dve", lambda e: e.tensor_tensor(out=out_ap, in0=t1, in1=t2, op=ALU.add),
                  reads=[("ht1", s), ("ht2", s)], writes=[out_key])


def load_tabs(K, tabs_d, kind, t0):
    cos = K.alloc(F32, TH)
    sn = K.alloc(F32, TH)
    dma(K, "sp", "tcos", cos, tabs_d[2 * kind][:, t0:t0 + TH], writes=["tab"])
    dma(K, "sp", "tsin", sn, tabs_d[2 * kind + 1][:, t0:t0 + TH], writes=["tab"])
    return cos, sn


def mla_proj(K, hT, t0, j, w_in, w_q_b, w_kv_b, tabs_d, QN, QP, KN, KP, VT, sub=None):
    S = K.S
    T = TH
    NH = 2
    m0 = K.mark()
    vm = V_MLA + 20 * j
    g_qa = K.vec[:, vm:vm + 12]
    g_kva = K.vec[:, vm + 12:vm + 16]
    g_qn = K.vec[:, vm + 16:vm + 17]
    g_qp = K.vec[:, vm + 17:vm + 18]
    g_kn = K.vec[:, vm + 18:vm + 19]
    g_kp = K.vec[:, vm + 19:vm + 20]
    cos, sn = load_tabs(K, tabs_d, 1, t0)
    if 'tabsonly' in MKX:
        S.barrier()
        K.reset(m0)
        return
    cqg = K.alloc(BF16, 12, T)
    ckvg = K.alloc(BF16, 4, T)
    r_cq = K.alloc(F32, T)
    r_cq2 = K.alloc(F32, T)
    r_kv = K.alloc(F32, T)
    r_kv2 = K.alloc(F32, T)
    r_kvc = K.alloc(F32, 8)
    colt = K.alloc(F32, 32)
    hn = HN(K)
    sqb = [K.alloc(BF16, T) for _ in range(2)]
    stg = [K.alloc(BF16, T) for _ in range(2)]
    wbuf = [K.alloc(BF16, 4608) for _ in range(2)]
    win = [w[:, 0:4096].rearrange("p (c n) -> p c n", n=128) for w in wbuf]
    wq = [w.rearrange("p (c n) -> p c n", n=384) for w in wbuf]
    Wv = w_in.rearrange("(c p) n -> p c n", p=128)
    def finstat(stat, r, r2, dim, key):
        for h in range(NH):
            S.add("act", lambda e, h=h: e.activation(out=r[:, h * 512:(h + 1) * 512], in_=stat[h][1][:, :], func=AF.Sqrt,
                                                       scale=1.0 / dim, bias=EPS), reads=[PS(stat[h])], writes=[key])
        S.add("dve", lambda e: e.reciprocal(out=r, in_=r), reads=[key], writes=[key])
        S.add("dve", lambda e: e.tensor_tensor(out=r2, in0=r, in1=r, op=ALU.mult), reads=[key], writes=[key])
    order = [("kv", c) for c in range(4)] + ([("pe", 0)] if 'nope' not in MKX else []) + [("cq", c) for c in range(12)]
    st_kv = K.reserve(2)
    st_col = K.reserve(1)[0]
    st_cq = None
    K.set_aux(2)
    for (kind, c) in order:
        s = K.nxt("win", 2)
        if kind == "pe":
            dma(K, "pool", "win%d" % s, win[s], Wv[:, :, 1984:2112], writes=[("win", s)])
        else:
            col = (1536 + c * 128) if kind == "kv" else c * 128
            dma(K, "pool", "win%d" % s, win[s], Wv[:, :, col:col + 128], writes=[("win", s)])
        if kind == "cq" and st_cq is None:
            finstat(st_kv, r_kv, r_kv2, 512, "rkv")
            S.add("act", lambda e: e.activation(out=colt, in_=st_col[1][:, 0:32], func=AF.Identity),
                  reads=[PS(st_col)], writes=["colt"])
            S.add("dve", lambda e: e.tensor_tensor(out=colt[:, 0:16], in0=colt[:, 0:16], in1=colt[:, 16:32], op=ALU.add),
                  reads=["colt"], writes=["colt"])
            S.add("dve", lambda e: e.tensor_tensor(out=colt[:, 0:8], in0=colt[:, 0:8], in1=colt[:, 8:16], op=ALU.add),
                  reads=["colt"], writes=["colt"])
            S.add("act", lambda e: e.activation(out=r_kvc, in_=colt[:, 0:8], func=AF.Sqrt, scale=1.0 / 512, bias=EPS),
                  reads=["colt"], writes=["rkvc"])
            S.add("dve", lambda e: e.reciprocal(out=r_kvc, in_=r_kvc), reads=["rkvc"], writes=["rkvc"])
            K.release(st_kv + [st_col])
            st_cq = K.reserve(2)
        pa = [K.psum() for _ in range(NH)]

        def mm(e, s=s, pa=pa, kind=kind):
            ins = None
            for k in range(DC):
                for h in range(NH):
                    if kind == "pe":
                        ins = e.matmul(pa[h][1][0:64, :], lhsT=win[s][:, k, 64:128], rhs=hT[:, k, h * 512:(h + 1) * 512],
                                       start=(k == 0), stop=(k == DC - 1))
                    else:
                        ins = e.matmul(pa[h][1][:, :], lhsT=win[s][:, k, :], rhs=hT[:, k, h * 512:(h + 1) * 512],
                                       start=(k == 0), stop=(k == DC - 1))
            return ins
        S.add("pe", mm, reads=[("win", s), "hT"], writes=[PS(p) for p in pa])
        if kind == "pe":
            for h in range(NH):
                q = K.nxt("stg", 2)
                sl_ = slice(h * 512, (h + 1) * 512)
                hn.run(pa[h], 64, g_kp[0:64], K.bd[0:64, 0:64], stg[q][0:64, sl_], ("stg", q, h),
                       rope=(cos[0:64, sl_], sn[0:64, sl_], K.perm_m[0:64, 0:64], "tab"), P=64)
                dma(K, "sp", "stgo%d" % q, KP[:, t0 + h * 512:t0 + (h + 1) * 512], stg[q][0:64, sl_],
                    reads=[("stg", q, h)], writes=[("stg", q, h)])
            continue
        q = K.nxt("sqb", 2)
        dst = ckvg if kind == "kv" else cqg
        gv = g_kva if kind == "kv" else g_qa
        stat = st_kv if kind == "kv" else st_cq
        nck = 4 if kind == "kv" else 12
        for h in range(NH):
            S.add("act", lambda e, q=q, h=h, pa=pa: e.activation(out=sqb[q][:, h * 512:(h + 1) * 512], in_=pa[h][1][:, :],
                                                                  func=AF.Square),
                  reads=[PS(pa[h])], writes=[("sqb", q, h)])
            S.add("act", lambda e, h=h, pa=pa, dst=dst, gv=gv, c=c: e.activation(
                out=dst[:, c, h * 512:(h + 1) * 512], in_=pa[h][1][:, :], func=AF.Identity, scale=gv[:, c:c + 1], bias=0.0),
                reads=[PS(pa[h]), "vec"], writes=[(kind, c, h)])

        def mmst(e, q=q, c=c, stat=stat, nck=nck, kind=kind):
            ins = None
            for h in range(NH):
                ins = e.matmul(stat[h][1][:, :], lhsT=K.ones_b, rhs=sqb[q][:, h * 512:(h + 1) * 512],
                               start=(c == 0), stop=(c == nck - 1))
            if kind == "kv" and 'nocol' not in MKX:
                for tt in range(8):
                    ins = e.matmul(st_col[1][:, c * 8 + tt:c * 8 + tt + 1], lhsT=sqb[q][:, tt * 128:(tt + 1) * 128],
                                   rhs=K.ones_b[:, 0:1], start=True, stop=True)
            return ins
        S.add("pe", mmst, reads=[("sqb", q, 0), ("sqb", q, 1)], writes=[PS(p) for p in stat] + ([PS(st_col)] if kind == "kv" else []))

    finstat(st_cq, r_cq, r_cq2, 1536, "rcq")
    K.release(st_cq)

    def fin():
        K.clear_aux()
        S.barrier()
        K.reset(m0)
    if sub == "pa":
        return fin()
    cq_keys = [("cq", c, h) for c in range(12) for h in range(2)]
    kv_keys = [("kv", c, h) for c in range(4) for h in range(2)]
    Wq = w_q_b.rearrange("(c p) n -> p c n", p=128)
    for hp in range(16):
        s = K.nxt("win", 2)
        dma(K, "pool", "win%d" % s, wq[s], Wq[:, :, hp * 384:(hp + 1) * 384], writes=[("win", s)])
        w3 = wq[s]
        for part in range(4):
            ispe = part % 2 == 1
            hd = 2 * hp + part // 2
            c0 = (part // 2) * 192 + (128 if ispe else 0)
            M = 64 if ispe else 128
            pa = [K.psum() for _ in range(NH)]

            def mm(e, s=s, pa=pa, c0=c0, M=M, w3=w3):
                ins = None
                for k in range(12):
                    for h in range(NH):
                        ins = e.matmul(pa[h][1][0:M, :], lhsT=w3[:, k, c0:c0 + M], rhs=cqg[:, k, h * 512:(h + 1) * 512],
                                       start=(k == 0), stop=(k == 11))
                return ins
            S.add("pe", mm, reads=[("win", s)] + cq_keys, writes=[PS(p) for p in pa])
            q = K.nxt("stg", 2)
            for h in range(NH):
                sl_ = slice(h * 512, (h + 1) * 512)
                if not ispe:
                    hn.run(pa[h], 128, g_qn, K.ones_b, stg[q][:, sl_], ("stg", q, h),
                           pre=(r_cq[:, sl_], r_cq2[:, sl_], "rcq"))
                else:
                    hn.run(pa[h], 64, g_qp[0:64], K.bd[0:64, 0:64], stg[q][0:64, sl_], ("stg", q, h),
                           pre=(r_cq[0:64, sl_], r_cq2[0:64, sl_], "rcq"),
                           rope=(cos[0:64, sl_], sn[0:64, sl_], K.perm_m[0:64, 0:64], "tab"), P=64)
            if not ispe:
                dma(K, "sp", "stgo%d" % q, QN[hd][:, t0:t0 + T], stg[q], reads=[("stg", q, 0), ("stg", q, 1)],
                    writes=[("stg", q, 0), ("stg", q, 1)])
            else:
                dma(K, "sp", "stgo%d" % q, QP[hd][:, t0:t0 + T], stg[q][0:64, :], reads=[("stg", q, 0), ("stg", q, 1)],
                    writes=[("stg", q, 0), ("stg", q, 1)])
    if sub == "pq":
        return fin()
    wk = [K.alloc(BF16, 4, 128) for _ in range(2)]
    Wk = w_kv_b.rearrange("(c p) n -> p c n", p=128)
    for hd in range(32):
        s = K.nxt("wk", 2)
        dma(K, "pool", "wk%d" % s, wk[s], Wk[:, :, hd * 256:hd * 256 + 128], writes=[("wk", s)])
        pa = [K.psum() for _ in range(NH)]

        def mm(e, s=s, pa=pa):
            ins = None
            for k in range(4):
                for h in range(NH):
                    ins = e.matmul(pa[h][1][:, :], lhsT=wk[s][:, k, :], rhs=ckvg[:, k, h * 512:(h + 1) * 512],
                                   start=(k == 0), stop=(k == 3))
            return ins
        S.add("pe", mm, reads=[("wk", s)] + kv_keys, writes=[PS(p) for p in pa])
        q = K.nxt("stg", 2)
        for h in range(NH):
            sl_ = slice(h * 512, (h + 1) * 512)
            hn.run(pa[h], 128, g_kn, K.ones_b, stg[q][:, sl_], ("stg", q, h), pre=(r_kv[:, sl_], r_kv2[:, sl_], "rkv"))
        dma(K, "sp", "stgo%d" % q, KN[hd][:, t0:t0 + T], stg[q], reads=[("stg", q, 0), ("stg", q, 1)],
            writes=[("stg", q, 0), ("stg", q, 1)])
    if sub == "pk":
        return fin()
    wv = [K.alloc(BF16, 4, 512) for _ in range(2)]
    vs = [K.alloc(BF16, 512) for _ in range(2)]
    Wv4 = w_kv_b.rearrange("(c p) (h two d) -> p c h two d", p=128, two=2, d=128)
    for hg in range(8):
        s = K.nxt("wv", 2)
        dma_multi(K, "pool", "wv%d" % s, [(wv[s][:, c, :].rearrange("p (h d) -> p h d", d=128), Wv4[:, c, hg * 4:(hg + 1) * 4, 1, :])
                                          for c in range(4)], writes=[("wv", s)])
        for tt in range(8):
            pb = K.psum()

            def mm(e, s=s, pb=pb, tt=tt):
                ins = None
                for k in range(4):
                    ins = e.matmul(pb[1][:, :], lhsT=ckvg[:, k, tt * 128:(tt + 1) * 128], rhs=wv[s][:, k, :],
                                   start=(k == 0), stop=(k == 3))
                return ins
            S.add("pe", mm, reads=[("wv", s)] + kv_keys, writes=[PS(pb)])
            q = K.nxt("vs", 2)
            S.add("act", lambda e, q=q, pb=pb, tt=tt: e.activation(out=vs[q], in_=pb[1][:, :], func=AF.Identity,
                                                                   scale=r_kvc[:, tt:tt + 1], bias=0.0),
                  reads=[PS(pb), "rkvc"], writes=[("vs", q)])
            dma(K, "sp", "vso%d" % q, VT[t0 + tt * 128:t0 + (tt + 1) * 128, hg * 512:(hg + 1) * 512], vs[q],
                reads=[("vs", q)], writes=[("vs", q)])
    K.clear_aux()
    S.barrier()
    K.reset(m0)


def mla_attn(K, QN, QP, KN, KP, VT, OT):
    S = K.S
    m0 = K.mark()
    scale = 192.0 ** -0.5
    kp = K.alloc(BF16, SEQ)
    dma(K, "sp", "kp", kp[0:64, :], KP, writes=["kp"])
    qn = [K.alloc(BF16, SEQ) for _ in range(2)]
    kn = [K.alloc(BF16, SEQ) for _ in range(2)]
    qp = [K.alloc(BF16, SEQ) for _ in range(2)]
    vv = [K.alloc(BF16, 16, 128) for _ in range(2)]
    ob = [K.alloc(BF16, SEQ) for _ in range(2)]
    E = [K.alloc(BF16, 512) for _ in range(3)]
    rd = [K.alloc(F32, 512) for _ in range(2)]
    VTv = VT.rearrange("(kt p) c -> p kt c", p=128)
    for hd in range(32):
        s = hd % 2
        ps_ = s
        pbase = 0
        dma(K, "sp", "aqn%d" % s, qn[s], QN[hd], writes=[("qn", s)])
        dma(K, "sp", "akn%d" % s, kn[s], KN[hd], writes=[("kn", s)])
        dma(K, "sp", "aqp%d" % s, qp[s][0:64, :], QP[hd], writes=[("qp", s)])
        dma(K, "sp", "avv%d" % s, vv[s], VTv[:, :, hd * 128:(hd + 1) * 128], writes=[("vv", s)])
        for qb in range(4):
            acc = K.reserve(2)
            nkt = 4 * qb + 4
            for kt in range(nkt):
                r = kt - 4 * qb
                q0 = 128 * r if r > 0 else 0
                sb = K.psum()

                def mms(e, s=s, ps_=ps_, sb=sb, kt=kt, qb=qb, q0=q0, pbase=pbase):
                    e.matmul(sb[1][:, q0:512], lhsT=kn[s][:, kt * 128:(kt + 1) * 128], rhs=qn[s][:, qb * 512 + q0:(qb + 1) * 512],
                             start=True, stop=False)
                    return e.matmul(sb[1][:, q0:512], lhsT=kp[pbase:pbase + 64, kt * 128:(kt + 1) * 128],
                                    rhs=qp[ps_][pbase:pbase + 64, qb * 512 + q0:(qb + 1) * 512], start=False, stop=True)
                S.add("pe", mms, reads=[("qn", s), ("kn", s), ("qp", ps_), "kp"], writes=[PS(sb)])
                ei = K.nxt("E", 3)
                S.add("act", lambda e, ei=ei, sb=sb, q0=q0: e.activation(out=E[ei][:, q0:512], in_=sb[1][:, q0:512],
                                                                         func=AF.Exp, scale=scale),
                      reads=[PS(sb)], writes=[("E", ei)])
                if r >= 0:
                    S.add("dve", lambda e, ei=ei, q0=q0: e.tensor_tensor(out=E[ei][:, q0:q0 + 128], in0=E[ei][:, q0:q0 + 128],
                                                                         in1=K.tri, op=ALU.mult),
                          reads=[("E", ei)], writes=[("E", ei)])

                def mmo(e, s=s, ei=ei, kt=kt, q0=q0, acc=acc, nkt=nkt):
                    e.matmul(acc[0][1][:, q0:512], lhsT=vv[s][:, kt, :], rhs=E[ei][:, q0:512],
                             start=(kt == 0), stop=(kt == nkt - 1))
                    return e.matmul(acc[1][1][:, q0:512], lhsT=K.ones_b, rhs=E[ei][:, q0:512],
                                    start=(kt == 0), stop=(kt == nkt - 1))
                S.add("pe", mmo, reads=[("E", ei), ("vv", s)], writes=[PS(acc[0]), PS(acc[1])])
            ri = K.nxt("rd", 2)
            S.add("act", lambda e, ri=ri, acc=acc: e.activation(out=rd[ri], in_=acc[1][1][:, :], func=AF.Identity),
                  reads=[PS(acc[1])], writes=[("rd", ri)])
            S.add("dve", lambda e, ri=ri: e.reciprocal(out=rd[ri], in_=rd[ri]), reads=[("rd", ri)], writes=[("rd", ri)])
            S.add("dve", lambda e, ri=ri, acc=acc, s=s, qb=qb: e.tensor_tensor(out=ob[s][:, qb * 512:(qb + 1) * 512],
                                                                                  in0=acc[0][1][:, :], in1=rd[ri], op=ALU.mult),
                  reads=[PS(acc[0]), ("rd", ri)], writes=[("ob", s, qb)])
            K.release(acc)
        dma(K, "sp", "aob%d" % s, OT[hd * 128:(hd + 1) * 128, :], ob[s], reads=[("ob", s, qb) for qb in range(4)],
            writes=[("ob", s, qb) for qb in range(4)])
    S.barrier()
    K.reset(m0)


DIL_R = (1, 4, 16)


def dil_proj(K, hT, t0, j, w_qkv, tabs_d, Qd, Kd, Vd):
    S = K.S
    T = TH
    NH = 2
    m0 = K.mark()
    vd = V_DIL + 6 * j
    cos, sn = load_tabs(K, tabs_d, 0, t0)
    hn = HN(K)
    K.set_aux(4)
    stg = [K.alloc(BF16, T) for _ in range(2)]
    wt = [K.alloc(BF16, DC, 256) for _ in range(2)]
    Wv = w_qkv.rearrange("(c p) n -> p c n", p=128)
    for sgi in range(2):
        for g in range(3):
            gain = K.vec[:, vd + 3 * sgi + g:vd + 3 * sgi + g + 1]
            for h2 in range(8):
                s = K.nxt("dwt", 2)
                col = ((sgi * 3 + g) * 16 + 2 * h2) * 128
                dma(K, "pool", "dwt%d" % s, wt[s], Wv[:, :, col:col + 256], writes=[("wt", s)])
                for hh in range(2):
                    pa = [K.psum() for _ in range(NH)]

                    def mm(e, s=s, pa=pa, hh=hh):
                        ins = None
                        for k in range(DC):
                            for h in range(NH):
                                ins = e.matmul(pa[h][1][:, :], lhsT=wt[s][:, k, hh * 128:(hh + 1) * 128],
                                               rhs=hT[:, k, h * 512:(h + 1) * 512], start=(k == 0), stop=(k == DC - 1))
                        return ins
                    S.add("pe", mm, reads=[("wt", s), "hT"], writes=[PS(p) for p in pa])
                    q = K.nxt("stg", 2)
                    for h in range(NH):
                        sl_ = slice(h * 512, (h + 1) * 512)
                        hn.run(pa[h], 128, gain, K.ones_b, stg[q][:, sl_], ("stg", q, h),
                               rope=(cos[:, sl_], sn[:, sl_], K.perm_d, "tab"))
                    dst = (Qd if sgi == 0 else Kd)[g * 16 + 2 * h2 + hh]
                    dma(K, "sp", "stgo%d" % q, dst[:, t0:t0 + T], stg[q], reads=[("stg", q, 0), ("stg", q, 1)],
                        writes=[("stg", q, 0), ("stg", q, 1)])
    K.clear_aux()
    S.barrier()
    K.reset(m0)
    wv = [K.alloc(BF16, DC, 512) for _ in range(2)]
    vs = [K.alloc(BF16, 512) for _ in range(2)]
    for cg in range(12):
        s = K.nxt("dwv", 2)
        col = 2 * 3 * 16 * 128 + cg * 512
        dma(K, "pool", "dwv%d" % s, wv[s], Wv[:, :, col:col + 512], writes=[("wv", s)])
        for tt in range(8):
            pb = K.psum()

            def mm(e, s=s, pb=pb, tt=tt):
                ins = None
                for k in range(DC):
                    ins = e.matmul(pb[1][:, :], lhsT=hT[:, k, tt * 128:(tt + 1) * 128], rhs=wv[s][:, k, :],
                                   start=(k == 0), stop=(k == DC - 1))
                return ins
            S.add("pe", mm, reads=[("wv", s), "hT"], writes=[PS(pb)])
            q = K.nxt("vs", 2)
            S.add("act", lambda e, q=q, pb=pb: e.activation(out=vs[q], in_=pb[1][:, :], func=AF.Identity),
                  reads=[PS(pb)], writes=[("vs", q)])
            dma(K, "sp", "vso%d" % q, Vd[t0 + tt * 128:t0 + (tt + 1) * 128, cg * 512:(cg + 1) * 512], vs[q],
                reads=[("vs", q)], writes=[("vs", q)])
    S.barrier()
    K.reset(m0)


def dil_attn(K, Qd, Kd, Vd, OT):
    S = K.S
    m0 = K.mark()
    scale = 128.0 ** -0.5
    accO = K.alloc(F32, SEQ)
    accD = K.alloc(F32, SEQ)
    qd = [K.alloc(BF16, SEQ) for _ in range(2)]
    kd = [K.alloc(BF16, SEQ) for _ in range(2)]
    vv = [K.alloc(BF16, 16, 128) for _ in range(2)]
    E = [K.alloc(BF16, 256) for _ in range(3)]
    ob = [K.alloc(BF16, SEQ) for _ in range(2)]
    for hh in range(16):
        for g in range(3):
            r = DIL_R[g]
            L = SEQ // r
            M = L // 128
            s = K.nxt("dq", 2)
            idx = g * 16 + hh
            dma(K, "sp", "dqd%d" % s, qd[s], Qd[idx], writes=[("qd", s)])
            dma(K, "sp", "dkd%d" % s, kd[s], Kd[idx], writes=[("kd", s)])
            vsrc = Vd[:, idx * 128:(idx + 1) * 128].rearrange("(m kj r) d -> kj r m d", kj=128, r=r)
            for rho in range(r):
                dma(K, "sp", "dvv%d" % s, vv[s][:, rho * M:(rho + 1) * M, :], vsrc[:, rho], writes=[("vv", s, rho)])
            qv = qd[s].rearrange("p (l r) -> p r l", r=r)
            kv = kd[s].rearrange("p (l r) -> p r l", r=r)
            aO = accO.rearrange("p (l r) -> p r l", r=r)
            aD = accD.rearrange("p (l r) -> p r l", r=r)
            blocks = [(rho, n) for rho in range(r) for n in range(M)]
            for c4 in range(4):
                acc = K.reserve(2)
                for bi in range(4):
                    rho, n = blocks[c4 * 4 + bi]
                    sb = K.psum()
                    c0 = 0 if n > 0 else 128

                    def mms(e, s=s, sb=sb, rho=rho, n=n, qv=qv, kv=kv):
                        if n > 0:
                            e.matmul(sb[1][:, 0:128], lhsT=kv[:, rho, (n - 1) * 128:n * 128], rhs=qv[:, rho, n * 128:(n + 1) * 128],
                                     start=True, stop=True)
                        return e.matmul(sb[1][:, 128:256], lhsT=kv[:, rho, n * 128:(n + 1) * 128],
                                        rhs=qv[:, rho, n * 128:(n + 1) * 128], start=True, stop=True)
                    S.add("pe", mms, reads=[("qd", s), ("kd", s)], writes=[PS(sb)])
                    ei = K.nxt("dE", 3)
                    S.add("act", lambda e, ei=ei, sb=sb, c0=c0: e.activation(out=E[ei][:, c0:256], in_=sb[1][:, c0:256],
                                                                             func=AF.Exp, scale=scale),
                          reads=[PS(sb)], writes=[("E", ei)])
                    S.add("dve", lambda e, ei=ei, c0=c0: e.tensor_tensor(out=E[ei][:, c0:256], in0=E[ei][:, c0:256],
                                                                         in1=K.dmask[:, c0:256], op=ALU.mult),
                          reads=[("E", ei)], writes=[("E", ei)])

                    def mmo(e, s=s, ei=ei, rho=rho, n=n, bi=bi, acc=acc, M=M):
                        cs = slice(bi * 128, (bi + 1) * 128)
                        if n > 0:
                            e.matmul(acc[0][1][:, cs], lhsT=vv[s][:, rho * M + n - 1, :], rhs=E[ei][:, 0:128], start=True, stop=False)
                        e.matmul(acc[0][1][:, cs], lhsT=vv[s][:, rho * M + n, :], rhs=E[ei][:, 128:256], start=(n == 0), stop=True)
                        if n > 0:
                            e.matmul(acc[1][1][:, cs], lhsT=K.ones_b, rhs=E[ei][:, 0:128], start=True, stop=False)
                        return e.matmul(acc[1][1][:, cs], lhsT=K.ones_b, rhs=E[ei][:, 128:256], start=(n == 0), stop=True)
                    S.add("pe", mmo, reads=[("E", ei)] + [("vv", s, rr) for rr in range(r)], writes=[PS(acc[0]), PS(acc[1])])
                if r == 1:
                    dO = aO[:, 0, c4 * 512:(c4 + 1) * 512]
                    dD = aD[:, 0, c4 * 512:(c4 + 1) * 512]
                    sO = acc[0][1][:, :]
                    sD = acc[1][1][:, :]
                elif r == 4:
                    dO = aO[:, c4, :]
                    dD = aD[:, c4, :]
                    sO = acc[0][1][:, :]
                    sD = acc[1][1][:, :]
                else:
                    dO = aO[:, c4 * 4:(c4 + 1) * 4, :]
                    dD = aD[:, c4 * 4:(c4 + 1) * 4, :]
                    sO = acc[0][1][:, :].rearrange("p (a b) -> p a b", b=128)
                    sD = acc[1][1][:, :].rearrange("p (a b) -> p a b", b=128)
                if g == 0:
                    S.add("dve", lambda e, dO=dO, sO=sO: e.tensor_copy(out=dO, in_=sO), reads=[PS(acc[0])], writes=["accO"])
                    S.add("dve", lambda e, dD=dD, sD=sD: e.tensor_copy(out=dD, in_=sD), reads=[PS(acc[1])], writes=["accD"])
                else:
                    S.add("dve", lambda e, dO=dO, sO=sO: e.tensor_tensor(out=dO, in0=dO, in1=sO, op=ALU.add),
                          reads=[PS(acc[0]), "accO"], writes=["accO"])
                    S.add("dve", lambda e, dD=dD, sD=sD: e.tensor_tensor(out=dD, in0=dD, in1=sD, op=ALU.add),
                          reads=[PS(acc[1]), "accD"], writes=["accD"])
                K.release(acc)
        o = hh % 2
        S.add("dve", lambda e: e.reciprocal(out=accD, in_=accD), reads=["accD"], writes=["accD"])
        S.add("dve", lambda e, o=o: e.tensor_tensor(out=ob[o], in0=accO, in1=accD, op=ALU.mult),
              reads=["accO", "accD"], writes=[("ob", o)])
        dma(K, "sp", "dob%d" % o, OT[hh * 128:(hh + 1) * 128, :], ob[o], reads=[("ob", o)], writes=[("ob", o)])
    S.barrier()
    K.reset(m0)


def build(n_layers=DEPTH, stop=None, declared=None):
    nc = bass.Bass("TRN2", target_bir_lowering=False)
    cache = {}

    def inp(name, shape, dt=F32):
        if name not in cache:
            cache[name] = nc.dram_tensor(name, list(shape), dt, kind="ExternalInput").ap()
            if declared is not None:
                declared.append(name)
        return cache[name]
    xT = inp("xT", [D, SEQ])
    posb = inp("posb", [128, SEQ], I32)
    vecs = inp("vecs", [128, NV])
    cmat = inp("cmat", [128, NCM])
    w_cond = inp("w_cond", [D, 512])
    w_mod = lambda i: inp("w_mod%d" % i, [512, 6 * D])
    mla_w_in = lambda j: inp("mla_w_in%d" % j, [D, 2112])
    mla_w_q_b = lambda j: inp("mla_w_q_b%d" % j, [1536, 6144])
    mla_w_kv_b = lambda j: inp("mla_w_kv_b%d" % j, [512, 8192])
    mla_w_o = lambda j: inp("mla_w_o%d" % j, [D, D])
    dil_w_qkv = lambda j: inp("dil_w_qkv%d" % j, [D, 18432])
    dil_w_o = lambda j: inp("dil_w_o%d" % j, [2048, D])
    ffn_wg = lambda i: inp("ffn_wg%d" % i, [D, HID])
    ffn_wu = lambda i: inp("ffn_wu%d" % i, [D, HID])
    ffn_wd = lambda i: inp("ffn_wd%d" % i, [HID, D])
    y = nc.dram_tensor("y", [D, SEQ], F32, kind="ExternalOutput").ap()

    def scr(name, shape, dt=BF16):
        return nc.dram_tensor(name, list(shape), dt).ap()
    tabs = scr("tabs", [4, 128, SEQ], F32)
    QN = scr("QN", [48, 128, SEQ])
    KN = scr("KN", [48, 128, SEQ])
    QP = scr("QP", [32, 64, SEQ])
    KP = scr("KP", [64, SEQ])
    VT = scr("VT", [SEQ, 6144])
    OT = scr("OT", [D, SEQ])

    with ExitStack() as st:
        K = Ctx(nc, st)
        S = K.S

        def body():
            setup(K, vecs, cmat)
            dma(K, "sp", "xcp", y, xT)
            rope_tables(K, posb, tabs)
            cond_embed(K, w_cond)
            if stop == "setup":
                return
            hT = K.alloc(BF16, DC, TH)
            mv = K.modv
            A_m, B_m, gt_m = mv[:, 0:DC], mv[:, DC:2 * DC], mv[:, 2 * DC:3 * DC]
            A_f, B_f, gt_f = mv[:, 3 * DC:4 * DC], mv[:, 4 * DC:5 * DC], mv[:, 5 * DC:6 * DC]
            VTm = VT[:, 0:4096]
            for li in range(n_layers):
                j = li // 2
                mod_layer(K, w_mod(li), li)
                if stop == "mod":
                    return
                if li % 2 == 0:
                    for th in range(2):
                        norm_phase(K, y, th * TH, A_m, B_m, hT)
                        if stop == "norm":
                            return
                        mla_proj(K, hT, th * TH, j, mla_w_in(j), mla_w_q_b(j), mla_w_kv_b(j), tabs, QN, QP, KN, KP, VTm, sub=stop)
                        if stop in ('pa', 'pq', 'pk'):
                            return
                    if stop == "proj":
                        return
                    mla_attn(K, QN, QP, KN, KP, VTm, OT)
                    if stop == "attn":
                        return
                    for th in range(2):
                        dma(K, "sp", "oth", hT, OT.rearrange("(c p) t -> p c t", p=128)[:, :, th * TH:(th + 1) * TH], writes=["hT"])
                        proj_residual(K, hT, DC, mla_w_o(j), gt_m, y, th * TH, "mo")
                else:
                    for th in range(2):
                        norm_phase(K, y, th * TH, A_m, B_m, hT)
                        dil_proj(K, hT, th * TH, j, dil_w_qkv(j), tabs, QN, KN, VT)
                    if stop == "proj":
                        return
                    dil_attn(K, QN, KN, VT, OT)
                    if stop == "attn":
                        return
                    for th in range(2):
                        dma(K, "sp", "oth", hT[:, 0:16, :], OT[0:2048, :].rearrange("(c p) t -> p c t", p=128)[:, :, th * TH:(th + 1) * TH],
                            writes=["hT"])
                        proj_residual(K, hT, 16, dil_w_o(j), gt_m, y, th * TH, "do")
                if stop == "mix" and li == n_layers - 1:
                    return
                for th in range(2):
                    norm_phase(K, y, th * TH, A_f, B_f, hT)
                    ffn_phase(K, hT, y, th * TH, ffn_wg(li), ffn_wu(li), ffn_wd(li), gt_f)
        body()
        S.barrier()
        S.emit(nc, st)
    return nc


def _arr(v):
    v = np.asarray(v, np.float32)
    return np.ascontiguousarray(v.reshape(-1, 128).T)


def make_consts():
    cm = np.zeros((128, NCM), np.float32)
    for m in range(128):
        cm[(m + 64) % 128, C_PERM_D + m] = 1.0
        blk = (m // 64) * 64
        cm[blk + ((m % 64) + 32) % 64, C_PERM_M + m] = 1.0
    cm[0:64, C_BD:C_BD + 64] = 1.0
    cm[64:128, C_BD + 64:C_BD + 128] = 1.0
    k = np.arange(128)[:, None]
    q = np.arange(128)[None, :]
    cm[:, C_TRI:C_TRI + 128] = (k <= q)
    cm[:, C_DMASK:C_DMASK + 128] = (k >= q)
    cm[:, C_DMASK + 128:C_DMASK + 256] = (k <= q)
    p = np.arange(128)
    invf_d = np.exp((p % 64).astype(np.float32) * np.float32(-2.0 * math.log(10000.0) / 128)).astype(np.float32)
    invf_m = np.exp((p % 32).astype(np.float32) * np.float32(-2.0 * math.log(10000.0) / 64)).astype(np.float32)
    sgn_d = np.where(p < 64, -1.0, 1.0).astype(np.float32)
    sgn_m = np.where((p % 64) < 32, -1.0, 1.0).astype(np.float32)
    return cm, np.stack([invf_d, invf_m, sgn_d, sgn_m], axis=1)


def make_in_map(b, I, cm, consts):
    vec = np.zeros((128, NV), np.float32)
    vec[:, V_C:V_C + 32] = _arr(I["c"][b])
    vec[:, V_BCOND:V_BCOND + 4] = _arr(I["b_cond"])
    for i in range(DEPTH):
        o = V_LAYER + 256 * i
        vec[:, o:o + 192] = _arr(I["b_mod"][i])
        vec[:, o + 192:o + 224] = _arr(I["g_mix_norm"][i])
        vec[:, o + 224:o + 256] = _arr(I["g_ffn_norm"][i])
    for j in range(2):
        o = V_MLA + 20 * j
        vec[:, o:o + 12] = _arr(I["mla_g_q_a"][j])
        vec[:, o + 12:o + 16] = _arr(I["mla_g_kv_a"][j])
        vec[:, o + 16] = I["mla_g_q_nope"][j]
        vec[:, o + 17] = np.tile(I["mla_g_q_pe"][j], 2)
        vec[:, o + 18] = I["mla_g_k_nope"][j]
        vec[:, o + 19] = np.tile(I["mla_g_k_pe"][j], 2)
        o = V_DIL + 6 * j
        vec[:, o:o + 3] = np.asarray(I["dil_g_q"][j]).T
        vec[:, o + 3:o + 6] = np.asarray(I["dil_g_k"][j]).T
    vec[:, V_CONST:V_CONST + 4] = consts
    m = {
        "xT": np.ascontiguousarray(np.asarray(I["x"][b]).T),
        "posb": np.ascontiguousarray(np.broadcast_to(np.asarray(I["positions"][b], np.int32)[None, :], (128, SEQ))),
        "vecs": vec, "cmat": cm, "w_cond": np.asarray(I["w_cond"]),
    }
    for i in range(DEPTH):
        m["w_mod%d" % i] = np.asarray(I["w_mod"][i])
        m["ffn_wg%d" % i] = np.asarray(I["ffn_w_gate"][i])
        m["ffn_wu%d" % i] = np.asarray(I["ffn_w_up"][i])
        m["ffn_wd%d" % i] = np.asarray(I["ffn_w_down"][i])
    for j in range(2):
        m["mla_w_in%d" % j] = np.asarray(I["mla_w_in"][j])
        m["mla_w_q_b%d" % j] = np.asarray(I["mla_w_q_b"][j])
        m["mla_w_kv_b%d" % j] = np.asarray(I["mla_w_kv_b"][j])
        m["mla_w_o%d" % j] = np.asarray(I["mla_w_o"][j])
        m["dil_w_qkv%d" % j] = np.asarray(I["dil_w_qkv"][j])
        m["dil_w_o%d" % j] = np.asarray(I["dil_w_o"][j])
    return m


def kernel(**inputs):
    I = inputs
    cm, consts = make_consts()
    nc = build(DEPTH)
    B = np.asarray(I["x"]).shape[0]
    in_maps = [make_in_map(b, I, cm, consts) for b in range(B)]
    res = run_bass_kernel_spmd(nc, in_maps, core_ids=list(range(B)))
    out = np.stack([np.ascontiguousarray(res.results[b]["y"].T) for b in range(B)], axis=0)
    return out.astype(np.float32)
```

```python
import math
import os
import numpy as np
MKX = os.environ.get('MKX', '')
from contextlib import ExitStack
import concourse.bass as bass
import concourse.mybir as mybir
from concourse.bass_utils import run_bass_kernel_spmd

F32 = mybir.dt.float32
BF16 = mybir.dt.bfloat16
I32 = mybir.dt.int32
AF = mybir.ActivationFunctionType
ALU = mybir.AluOpType

D = 4096
SEQ = 2048
TH = 1024
DC = D // 128
HID = 11008
DEPTH = 4
EPS = 1e-6
ARENA_WORDS = 49152
N_ACTIVE = 4


class Op:
    __slots__ = ("eng", "fn", "deps", "signal", "ev", "dma_key", "ndma", "idx", "inc")

    def __init__(self, eng, fn, dma_key=None, ndma=1, inc=16):
        self.eng = eng
        self.fn = fn
        self.deps = []
        self.signal = False
        self.ev = None
        self.dma_key = dma_key
        self.ndma = ndma
        self.inc = inc


class Sched:
    def __init__(self):
        self.ops = []
        self.last_write = {}
        self.readers = {}
        self.last_on = {}
        self.pending_dma = []

    def add(self, eng, fn, reads=(), writes=(), dma_key=None, ndma=1, inc=16):
        op = Op(eng, fn, dma_key, ndma, inc)
        op.idx = len(self.ops)
        deps = set()
        for r in reads:
            w = self.last_write.get(r)
            if w is not None:
                deps.add(w)
        for wkey in writes:
            w = self.last_write.get(wkey)
            if w is not None:
                deps.add(w)
            for rd in self.readers.get(wkey, ()):
                deps.add(rd)
        op.deps = sorted(deps)
        for wkey in writes:
            self.last_write[wkey] = op.idx
            self.readers[wkey] = []
        for r in reads:
            if r in writes:
                continue
            self.readers.setdefault(r, []).append(op.idx)
        self.ops.append(op)
        self.last_on[eng] = op.idx
        if dma_key is not None:
            self.pending_dma.append(op.idx)
        return op

    def barrier(self, engines=("pe", "act", "dve", "pool", "sp")):
        deps = sorted(set(self.last_on.values()) | set(self.pending_dma))
        self.pending_dma = []
        self.last_write = {}
        self.readers = {}
        for eng in engines:
            op = Op(eng, None)
            op.idx = len(self.ops)
            op.deps = list(deps)
            self.ops.append(op)
            self.last_on[eng] = op.idx

    def emit(self, nc, stack):
        ops = self.ops
        nosame = 'nosame' in MKX
        for op in ops:
            if nosame and op.fn is not None and op.dma_key is None:
                op.deps = [d for d in op.deps if not (ops[d].eng == op.eng and ops[d].dma_key is None and ops[d].fn is not None)]
            for d in op.deps:
                ops[d].signal = True
            if op.dma_key is not None:
                op.signal = True
        cnt = {}
        order = []
        for op in ops:
            if not op.signal or op.fn is None:
                continue
            key = ("dma", op.dma_key) if op.dma_key is not None else ("eng", op.eng)
            if key not in cnt:
                cnt[key] = 0
                order.append(key)
            cnt[key] += op.inc * op.ndma if op.dma_key is not None else 1
            op.ev = (key, cnt[key])
        sems = {}
        for key in order:
            sems[key] = stack.enter_context(nc.semaphore("s_%s_%s" % key))
        self.nsems = len(sems)
        streams = {}
        for op in ops:
            streams.setdefault(op.eng, []).append(op)
        block = stack.enter_context(nc.Block())

        def run(e, engname):
            seen = {}
            for op in streams.get(engname, []):
                need = {}
                for d in op.deps:
                    p = ops[d]
                    if p.ev is None:
                        continue
                    k, c = p.ev
                    if need.get(k, 0) < c:
                        need[k] = c
                for k, c in need.items():
                    if seen.get(k, 0) < c:
                        e.wait_ge(sems[k], c)
                        seen[k] = c
                if op.fn is None:
                    continue
                if op.dma_key is not None:
                    op.fn(e, sems[("dma", op.dma_key)])
                else:
                    ins = op.fn(e)
                    if op.signal:
                        ins.then_inc(sems[("eng", op.eng)], 1)

        if "pe" in streams:
            @block.tensor
            def _(e):
                run(e, "pe")
        if "act" in streams:
            @block.scalar
            def _(e):
                run(e, "act")
        if "dve" in streams:
            @block.vector
            def _(e):
                run(e, "dve")
        if "pool" in streams:
            @block.gpsimd
            def _(e):
                run(e, "pool")
        if "sp" in streams:
            @block.sync
            def _(e):
                run(e, "sp")


class Ctx:
    def __init__(self, nc, st):
        self.nc = nc
        self.st = st
        self.S = Sched()
        self.arena = st.enter_context(nc.sbuf_tensor("arena", [128, ARENA_WORDS], F32))
        self.ps = [st.enter_context(nc.psum_tensor("ps%d" % i, [128, 512], F32)) for i in range(8)]
        self.ps_free = list(range(8))
        self.off = 0
        self.rot = {}
        self.aux = []

    def mark(self):
        return self.off

    def reset(self, off):
        self.off = off

    def alloc(self, dtype, *free):
        esz = 4 if dtype in (F32, I32) else 2
        n = 1
        for f in free:
            n *= f
        nbytes = (n * esz + 31) // 32 * 32
        off = self.off
        self.off += nbytes
        assert self.off <= ARENA_WORDS * 4, "arena overflow %d" % self.off
        v = self.arena[:, off // 4:(off + nbytes) // 4]
        if dtype != F32:
            v = v.bitcast(dtype)
        v = v[:, 0:n]
        if len(free) == 2:
            v = v.rearrange("p (a b) -> p a b", b=free[1])
        elif len(free) == 3:
            v = v.rearrange("p (a b c) -> p a b c", b=free[1], c=free[2])
        return v

    def psum(self):
        i = self.ps_free.pop(0)
        self.ps_free.append(i)
        return i, self.ps[i]

    def set_aux(self, n):
        self.aux = [self.ps_free.pop(0) for _ in range(n)]

    def clear_aux(self):
        self.ps_free.extend(self.aux)
        self.aux = []

    def psum_aux(self):
        i = self.aux.pop(0)
        self.aux.append(i)
        return i, self.ps[i]

    def reserve(self, n):
        out = []
        for _ in range(n):
            i = self.ps_free.pop(0)
            out.append((i, self.ps[i]))
        return out

    def release(self, banks):
        for i, _ in banks:
            self.ps_free.append(i)

    def nxt(self, name, n):
        v = self.rot.get(name, 0)
        self.rot[name] = v + 1
        return v % n


def dma(K, eng, key, out, in_, reads=(), writes=()):
    def fn(e, sem):
        e.dma_start(out=out, in_=in_).then_inc(sem, 16)
    K.S.add(eng, fn, reads=reads, writes=writes, dma_key=key, ndma=1)


def dma_multi(K, eng, key, pairs, reads=(), writes=()):
    def fn(e, sem):
        for (o, i) in pairs:
            e.dma_start(out=o, in_=i).then_inc(sem, 16)
    K.S.add(eng, fn, reads=reads, writes=writes, dma_key=key, ndma=len(pairs))


def PS(b):
    return ("ps", b[0])


V_C = 0
V_BCOND = 32
V_LAYER = 36
V_MLA = V_LAYER + 4 * 256
V_DIL = V_MLA + 40
V_CONST = V_DIL + 12
NV = V_CONST + 4
C_PERM_D = 0
C_PERM_M = 128
C_BD = 256
C_TRI = 384
C_DMASK = 512
NCM = 768


def setup(K, vecs_d, cmat_d):
    S = K.S
    K.vec = K.alloc(F32, NV)
    dma(K, "sp", "vec", K.vec, vecs_d, writes=["vec"])
    K.ones_f = K.alloc(F32, 128)
    K.ones_b = K.alloc(BF16, 128)
    K.cm = K.alloc(BF16, NCM)
    K.modv = K.alloc(F32, 6 * DC)
    K.e_b = K.alloc(BF16, 4)
    S.add("dve", lambda e: e.memset(K.ones_f, 1.0), writes=["ones_f"])
    S.add("dve", lambda e: e.memset(K.ones_b, 1.0), writes=["ones_b"])
    m0 = K.mark()
    cmf = K.alloc(F32, NCM)
    dma(K, "sp", "cmf", cmf, cmat_d, writes=["cmf"])
    S.add("dve", lambda e: e.tensor_copy(out=K.cm, in_=cmf), reads=["cmf"], writes=["cm"])
    S.barrier()
    K.reset(m0)
    K.perm_d = K.cm[:, C_PERM_D:C_PERM_D + 128]
    K.perm_m = K.cm[:, C_PERM_M:C_PERM_M + 128]
    K.bd = K.cm[:, C_BD:C_BD + 128]
    K.tri = K.cm[:, C_TRI:C_TRI + 128]
    K.dmask = K.cm[:, C_DMASK:C_DMASK + 256]


def rope_tables(K, posb_d, tabs_d):
    S = K.S
    m0 = K.mark()
    pi = K.alloc(I32, SEQ)
    pf = K.alloc(F32, SEQ)
    ang = K.alloc(F32, SEQ)
    z = K.alloc(F32, SEQ)
    a2 = K.alloc(F32, SEQ)
    o = K.alloc(F32, SEQ)
    dma(K, "sp", "pos", pi, posb_d, writes=["pi"])
    S.add("dve", lambda e: e.tensor_copy(out=pf, in_=pi), reads=["pi"], writes=["pf"])
    TWO_PI = 2.0 * math.pi
    for kind in range(2):
        invf = K.vec[:, V_CONST + kind:V_CONST + kind + 1]
        sgn = K.vec[:, V_CONST + 2 + kind:V_CONST + 3 + kind]
        S.add("dve", lambda e, invf=invf: e.tensor_scalar(out=ang, in0=pf, scalar1=invf, scalar2=None, op0=ALU.mult),
              reads=["pf", "vec"], writes=["ang"])
        for cs in range(2):
            shift = 0.5 * math.pi if cs == 0 else 0.0
            S.add("dve", lambda e, shift=shift: e.tensor_scalar(out=a2, in0=ang, scalar1=shift, scalar2=None, op0=ALU.add),
                  reads=["ang"], writes=["a2"])
            S.add("dve", lambda e: e.tensor_scalar(out=z, in0=a2, scalar1=1.0 / TWO_PI, scalar2=None, op0=ALU.mult),
                  reads=["a2"], writes=["z"])
            S.add("dve", lambda e: e.tensor_copy(out=pi, in_=z), reads=["z", "pf"], writes=["pi"])
            S.add("dve", lambda e: e.tensor_copy(out=z, in_=pi), reads=["pi"], writes=["z"])
            S.add("dve", lambda e: e.scalar_tensor_tensor(out=a2, in0=z, scalar=-TWO_PI, in1=a2, op0=ALU.mult, op1=ALU.add),
                  reads=["z", "a2"], writes=["a2"])
            S.add("dve", lambda e: e.tensor_scalar(out=z, in0=a2, scalar1=math.pi, scalar2=TWO_PI, op0=ALU.is_gt, op1=ALU.mult),
                  reads=["a2"], writes=["z"])
            S.add("dve", lambda e: e.tensor_tensor(out=z, in0=a2, in1=z, op=ALU.subtract), reads=["a2", "z"], writes=["z"])
            if cs == 0:
                S.add("act", lambda e: e.activation(out=o, in_=z, func=AF.Sin), reads=["z"], writes=["o"])
            else:
                S.add("act", lambda e, sgn=sgn: e.activation(out=z, in_=z, func=AF.Sin), reads=["z"], writes=["z"])
                S.add("dve", lambda e, sgn=sgn: e.tensor_scalar(out=o, in0=z, scalar1=sgn, scalar2=None, op0=ALU.mult),
                      reads=["z", "vec"], writes=["o"])
            dma(K, "sp", "tabo", tabs_d[2 * kind + cs], o, reads=["o"], writes=[("tab", 2 * kind + cs)])
    S.barrier()
    K.reset(m0)


def cond_embed(K, w_cond_d):
    S = K.S
    m0 = K.mark()
    wt = K.alloc(BF16, DC, 512)
    cb = K.alloc(BF16, DC)
    ef = K.alloc(F32, 4)
    dma(K, "pool", "wcond", wt, w_cond_d.rearrange("(c p) n -> p c n", p=128), writes=["wt"])
    S.add("dve", lambda e: e.tensor_copy(out=cb, in_=K.vec[:, V_C:V_C + DC]), reads=["vec"], writes=["cb"])
    b = K.psum()

    def mm(e):
        ins = None
        for rc in range(4):
            for k in range(DC):
                ins = e.matmul(b[1][:, rc:rc + 1], lhsT=wt[:, k, rc * 128:(rc + 1) * 128], rhs=cb[:, k:k + 1],
                               start=(k == 0), stop=(k == DC - 1))
        return ins
    S.add("pe", mm, reads=["wt", "cb"], writes=[PS(b)])
    S.add("dve", lambda e: e.tensor_tensor(out=ef, in0=b[1][:, 0:4], in1=K.vec[:, V_BCOND:V_BCOND + 4], op=ALU.add),
          reads=[PS(b), "vec"], writes=["ef"])
    S.add("act", lambda e: e.activation(out=K.e_b, in_=ef, func=AF.Silu), reads=["ef"], writes=["e_b"])
    S.barrier()
    K.reset(m0)


def mod_layer(K, w_mod_d, li):
    S = K.S
    m0 = K.mark()
    wts = [K.alloc(BF16, 4, 2048) for _ in range(2)]
    mod = K.alloc(F32, 6 * DC)
    wv = w_mod_d.rearrange("(c p) n -> p c n", p=128)
    b = K.reserve(1)[0]
    for t in range(12):
        s = t % 2
        dma(K, "pool", "wmod%d" % s, wts[s], wv[:, :, t * 2048:(t + 1) * 2048], writes=[("wm", s)])

        def mm(e, s=s, t=t):
            ins = None
            for jj in range(16):
                j = t * 16 + jj
                for rc in range(4):
                    ins = e.matmul(b[1][:, j:j + 1], lhsT=wts[s][:, rc, jj * 128:(jj + 1) * 128],
                                   rhs=K.e_b[:, rc:rc + 1], start=(rc == 0), stop=(rc == 3))
            return ins
        S.add("pe", mm, reads=[("wm", s), "e_b"], writes=[PS(b)])
    vb = V_LAYER + li * 256
    S.add("dve", lambda e: e.tensor_tensor(out=mod, in0=b[1][:, 0:6 * DC], in1=K.vec[:, vb:vb + 192], op=ALU.add),
          reads=[PS(b), "vec"], writes=["mod"])
    mv = K.modv

    def fin(e):
        e.scalar_tensor_tensor(out=mv[:, 0:DC], in0=mod[:, DC:2 * DC], scalar=1.0, in1=K.vec[:, vb + 192:vb + 224],
                               op0=ALU.add, op1=ALU.mult)
        e.tensor_copy(out=mv[:, DC:2 * DC], in_=mod[:, 0:DC])
        e.tensor_copy(out=mv[:, 2 * DC:3 * DC], in_=mod[:, 2 * DC:3 * DC])
        e.scalar_tensor_tensor(out=mv[:, 3 * DC:4 * DC], in0=mod[:, 4 * DC:5 * DC], scalar=1.0,
                               in1=K.vec[:, vb + 224:vb + 256], op0=ALU.add, op1=ALU.mult)
        e.tensor_copy(out=mv[:, 4 * DC:5 * DC], in_=mod[:, 3 * DC:4 * DC])
        return e.tensor_copy(out=mv[:, 5 * DC:6 * DC], in_=mod[:, 5 * DC:6 * DC])
    S.add("dve", fin, reads=["mod", "vec"], writes=["modv"])
    K.release([b])
    S.barrier()
    K.reset(m0)


def norm_phase(K, x_d, t0, A, B, hT):
    S = K.S
    T = TH
    NH = T // 512
    m0 = K.mark()
    xs = [K.alloc(F32, T) for _ in range(2)]
    sq = [K.alloc(F32, T) for _ in range(2)]
    rstd = K.alloc(F32, T)
    pst = K.reserve(NH)
    xv = x_d.rearrange("(c p) t -> c p t", p=128)
    for c in range(DC):
        s = c % 2
        dma(K, "sp", "nxs%d" % s, xs[s], xv[c][:, t0:t0 + T], writes=[("xs", s)])
        S.add("act", lambda e, s=s: e.activation(out=sq[s], in_=xs[s], func=AF.Square),
              reads=[("xs", s)], writes=[("sq", s)])

        def mm(e, s=s, c=c):
            ins = None
            for h in range(NH):
                ins = e.matmul(pst[h][1][:, :], lhsT=K.ones_f, rhs=sq[s][:, h * 512:(h + 1) * 512],
                               start=(c == 0), stop=(c == DC - 1))
            return ins
        S.add("pe", mm, reads=[("sq", s)], writes=[PS(p) for p in pst])
    for h in range(NH):
        S.add("act", lambda e, h=h: e.activation(out=rstd[:, h * 512:(h + 1) * 512], in_=pst[h][1][:, :],
                                                   func=AF.Ln, scale=1.0 / D, bias=EPS),
              reads=[PS(pst[h])], writes=[("rs", h)])
        S.add("act", lambda e, h=h: e.activation(out=rstd[:, h * 512:(h + 1) * 512], in_=rstd[:, h * 512:(h + 1) * 512],
                                                   func=AF.Exp, scale=-0.5),
              reads=[("rs", h)], writes=[("rs", h)])
    for c in range(DC):
        s = c % 2
        dma(K, "sp", "nxs%d" % s, xs[s], xv[c][:, t0:t0 + T], writes=[("xs", s)])
        S.add("dve", lambda e, s=s: e.tensor_tensor(out=sq[s], in0=xs[s], in1=rstd, op=ALU.mult),
              reads=[("xs", s)] + [("rs", h) for h in range(NH)], writes=[("sq", s)])
        S.add("act", lambda e, s=s, c=c: e.activation(out=hT[:, c, :], in_=sq[s], func=AF.Identity,
                                                        scale=A[:, c:c + 1], bias=B[:, c:c + 1]),
              reads=[("sq", s), "modv"], writes=[("hT", c)])
    K.release(pst)
    S.barrier()
    K.reset(m0)


def proj_residual(K, hT, KC, W_d, gt, x_d, t0, tag):
    S = K.S
    T = TH
    NH = T // 512
    m0 = K.mark()
    wd = [K.alloc(BF16, KC, 256) for _ in range(2)]
    xo = [K.alloc(F32, T) for _ in range(2)]
    Wv = W_d.rearrange("(c p) n -> p c n", p=128)
    xv = x_d.rearrange("(c p) t -> c p t", p=128)
    for n2 in range(DC // 2):
        s = n2 % 2
        dma(K, "pool", tag + "w%d" % s, wd[s], Wv[:, :, n2 * 256:(n2 + 1) * 256], writes=[("wd", s)])
        for nn in range(2):
            n = 2 * n2 + nn
            po = [K.psum() for _ in range(NH)]

            def mm2(e, s=s, nn=nn, po=po):
                ins = None
                for k in range(KC):
                    for h in range(NH):
                        ins = e.matmul(po[h][1][:, :], lhsT=wd[s][:, k, nn * 128:(nn + 1) * 128],
                                       rhs=hT[:, k, h * 512:(h + 1) * 512], start=(k == 0), stop=(k == KC - 1))
                return ins
            S.add("pe", mm2, reads=[("wd", s), "hT"], writes=[PS(p) for p in po])
            q = n % 2
            dma(K, "sp", tag + "xi%d" % q, xo[q], xv[n][:, t0:t0 + T], writes=[("xo", q)])
            for h in range(NH):
                S.add("dve", lambda e, q=q, h=h, po=po, n=n: e.scalar_tensor_tensor(
                    out=xo[q][:, h * 512:(h + 1) * 512], in0=po[h][1][:, :], scalar=gt[:, n:n + 1],
                    in1=xo[q][:, h * 512:(h + 1) * 512], op0=ALU.mult, op1=ALU.add),
                    reads=[PS(po[h]), ("xo", q), "modv"], writes=[("xo", q)])
            dma(K, "sp", tag + "xo%d" % q, xv[n][:, t0:t0 + T], xo[q], reads=[("xo", q)])
    S.barrier()
    K.reset(m0)


def ffn_phase(K, hT, x_d, t0, Wg, Wu, Wd, gt, GJ=22):
    S = K.S
    T = TH
    H = HID
    HC = H // 128
    NH = T // 512
    m0 = K.mark()
    wg = [K.alloc(BF16, DC, 128) for _ in range(2)]
    wu = [K.alloc(BF16, DC, 128) for _ in range(2)]
    act = K.alloc(BF16, GJ, T)
    wd = [K.alloc(BF16, GJ, 256) for _ in range(2)]
    sl = [K.alloc(F32, 512) for _ in range(2)]
    xo = [K.alloc(F32, T) for _ in range(2)]
    Wgv = Wg.rearrange("(c p) n -> p c n", p=128)
    Wuv = Wu.rearrange("(c p) n -> p c n", p=128)
    Wdv = Wd.rearrange("(c p) n -> p c n", p=128)
    xv = x_d.rearrange("(c p) t -> c p t", p=128)
    groups = [(j0, min(j0 + GJ, HC)) for j0 in range(0, HC, GJ)]
    for (j0, j1) in groups:
        nj = j1 - j0
        for j in range(j0, j1):
            s = K.nxt("fj", 2)
            dma(K, "pool", "fwg%d" % s, wg[s], Wgv[:, :, j * 128:(j + 1) * 128], writes=[("wg", s)])
            dma(K, "pool", "fwu%d" % s, wu[s], Wuv[:, :, j * 128:(j + 1) * 128], writes=[("wu", s)])
            pg = [K.psum() for _ in range(NH)]
            pu = [K.psum() for _ in range(NH)]

            def mm(e, s=s, pg=pg, pu=pu):
                ins = None
                for k in range(DC):
                    for h in range(NH):
                        ins = e.matmul(pg[h][1][:, :], lhsT=wg[s][:, k, :], rhs=hT[:, k, h * 512:(h + 1) * 512],
                                       start=(k == 0), stop=(k == DC - 1))
                for k in range(DC):
                    for h in range(NH):
                        ins = e.matmul(pu[h][1][:, :], lhsT=wu[s][:, k, :], rhs=hT[:, k, h * 512:(h + 1) * 512],
                                       start=(k == 0), stop=(k == DC - 1))
                return ins
            S.add("pe", mm, reads=[("wg", s), ("wu", s), "hT"], writes=[PS(p) for p in pg + pu])
            for h in range(NH):
                q = K.nxt("fsl", 2)
                S.add("act", lambda e, q=q, h=h, pg=pg: e.activation(out=sl[q], in_=pg[h][1][:, :], func=AF.Silu),
                      reads=[PS(pg[h])], writes=[("sl", q)])
                S.add("dve", lambda e, q=q, h=h, pu=pu, jj=j - j0: e.tensor_tensor(
                    out=act[:, jj, h * 512:(h + 1) * 512], in0=sl[q], in1=pu[h][1][:, :], op=ALU.mult),
                    reads=[("sl", q), PS(pu[h])], writes=[("act", j - j0, h)])
        for n2 in range(DC // 2):
            s = K.nxt("fn", 2)
            dma(K, "pool", "fwd%d" % s, wd[s][:, 0:nj, :], Wdv[:, j0:j1, n2 * 256:(n2 + 1) * 256], writes=[("wd", s)])
            for nn in range(2):
                n = 2 * n2 + nn
                po = [K.psum() for _ in range(NH)]

                def mm2(e, s=s, nn=nn, po=po, nj=nj):
                    ins = None
                    for jj in range(nj):
                        for h in range(NH):
                            ins = e.matmul(po[h][1][:, :], lhsT=wd[s][:, jj, nn * 128:(nn + 1) * 128],
                                           rhs=act[:, jj, h * 512:(h + 1) * 512],
                                           start=(jj == 0), stop=(jj == nj - 1))
                    return ins
                S.add("pe", mm2, reads=[("wd", s)] + [("act", jj, h) for jj in range(nj) for h in range(NH)],
                      writes=[PS(p) for p in po])
                q = K.nxt("fxo", 2)
                dma(K, "sp", "fxi%d" % q, xo[q], xv[n][:, t0:t0 + T], reads=[("xd", n)], writes=[("xo", q)])
                for h in range(NH):
                    S.add("dve", lambda e, q=q, h=h, po=po, n=n: e.scalar_tensor_tensor(
                        out=xo[q][:, h * 512:(h + 1) * 512], in0=po[h][1][:, :], scalar=gt[:, n:n + 1],
                        in1=xo[q][:, h * 512:(h + 1) * 512], op0=ALU.mult, op1=ALU.add),
                        reads=[PS(po[h]), ("xo", q), "modv"], writes=[("xo", q)])
                dma(K, "sp", "fxo%d" % q, xv[n][:, t0:t0 + T], xo[q], reads=[("xo", q)], writes=[("xd", n)])
    S.barrier()
    K.reset(m0)


class HN:
    def __init__(self, K, nset=2):
        self.K = K
        self.nset = nset
        self.sq = [K.alloc(BF16, 512) for _ in range(nset)]
        self.rs = [K.alloc(F32, 512) for _ in range(nset)]
        self.xb = [K.alloc(BF16, 512) for _ in range(nset)]
        self.t1 = [K.alloc(F32, 512) for _ in range(nset)]
        self.t2 = [K.alloc(F32, 512) for _ in range(nset)]

    def run(self, pb, dim, gain, ones_mat, out_ap, out_key, pre=None, rope=None, P=128):
        self.run2([dict(pb=pb, dim=dim, gain=gain, ones=ones_mat, out=out_ap, key=out_key, pre=pre, rope=rope, P=P)])

    def run2(self, items):
        K = self.K
        S = K.S
        for it in items:
            s = K.nxt("hn", self.nset)
            P = it["P"]
            it["s"] = s
            it["sq"], it["rs"], it["xb"], it["t1"], it["t2"] = (self.sq[s][0:P], self.rs[s][0:P], self.xb[s][0:P],
                                                                 self.t1[s][0:P], self.t2[s][0:P])
            it["pbv"] = it["pb"][1][0:P, :]
        for it in items:
            S.add("act", lambda e, it=it: e.activation(out=it["sq"], in_=it["pbv"], func=AF.Square),
                  reads=[PS(it["pb"])], writes=[("hsq", it["s"])])
        for it in items:
            it["ss"] = K.psum_aux()
            S.add("pe", lambda e, it=it: e.matmul(it["ss"][1][0:it["P"], :], lhsT=it["ones"], rhs=it["sq"], start=True, stop=True),
                  reads=[("hsq", it["s"])], writes=[PS(it["ss"])])
        for it in items:
            s = it["s"]
            if it["pre"] is not None:
                S.add("dve", lambda e, it=it: e.tensor_tensor(out=it["rs"], in0=it["ss"][1][0:it["P"], :], in1=it["pre"][1], op=ALU.mult),
                      reads=[PS(it["ss"]), it["pre"][2]], writes=[("hrs", s)])
        for it in items:
            s = it["s"]
            if it["pre"] is not None:
                S.add("act", lambda e, it=it: e.activation(out=it["rs"], in_=it["rs"], func=AF.Ln, scale=1.0 / it["dim"], bias=EPS),
                      reads=[("hrs", s)], writes=[("hrs", s)])
            else:
                S.add("act", lambda e, it=it: e.activation(out=it["rs"], in_=it["ss"][1][0:it["P"], :], func=AF.Ln,
                                                           scale=1.0 / it["dim"], bias=EPS),
                      reads=[PS(it["ss"])], writes=[("hrs", s)])
        for it in items:
            s = it["s"]
            S.add("act", lambda e, it=it: e.activation(out=it["rs"], in_=it["rs"], func=AF.Exp, scale=-0.5),
                  reads=[("hrs", s)], writes=[("hrs", s)])
        for it in items:
            s = it["s"]
            if it["pre"] is not None:
                S.add("dve", lambda e, it=it: e.tensor_tensor(out=it["rs"], in0=it["rs"], in1=it["pre"][0], op=ALU.mult),
                      reads=[("hrs", s), it["pre"][2]], writes=[("hrs", s)])
        for it in items:
            s = it["s"]
            if it["rope"] is None:
                S.add("dve", lambda e, it=it: e.scalar_tensor_tensor(out=it["out"], in0=it["pbv"], scalar=it["gain"], in1=it["rs"],
                                                                     op0=ALU.mult, op1=ALU.mult),
                      reads=[PS(it["pb"]), ("hrs", s), "vec"], writes=[it["key"]])
            else:
                S.add("dve", lambda e, it=it: e.scalar_tensor_tensor(out=it["xb"], in0=it["pbv"], scalar=it["gain"], in1=it["rs"],
                                                                     op0=ALU.mult, op1=ALU.mult),
                      reads=[PS(it["pb"]), ("hrs", s), "vec"], writes=[("hxb", s)])
        ritems = [it for it in items if it["rope"] is not None]
        for it in ritems:
            it["rot"] = K.psum_aux()
            S.add("pe", lambda e, it=it: e.matmul(it["rot"][1][0:it["P"], :], lhsT=it["rope"][2], rhs=it["xb"], start=True, stop=True),
                  reads=[("hxb", it["s"])], writes=[PS(it["rot"])])
        for it in ritems:
            s = it["s"]
            S.add("dve", lambda e, it=it: e.tensor_tensor(out=it["t1"], in0=it["xb"], in1=it["rope"][0], op=ALU.mult),
                  reads=[("hxb", s), it["rope"][3]], writes=[("ht1", s)])
        for it in ritems:
            s = it["s"]
            S.add("dve", lambda e, it=it: e.tensor_tensor(out=it["t2"], in0=it["rot"][1][0:it["P"], :], in1=it["rope"][1], op=ALU.mult),
                  reads=[PS(it["rot"]), it["rope"][3]], writes=[("ht2", s)])
        for it in ritems:
            s = it["s"]
            S.add("dve", lambda e, it=it: e.tensor_tensor(out=it["out"], in0=it["t1"], in1=it["t2"], op=ALU.add),
                  reads=[("ht1", s), ("ht2", s)], writes=[it["key"]])


def load_tabs(K, tabs_d, kind, t0):
    cos = K.alloc(F32, TH)
    sn = K.alloc(F32, TH)
    dma(K, "sp", "tcos", cos, tabs_d[2 * kind][:, t0:t0 + TH], writes=["tab"])
    dma(K, "sp", "tsin", sn, tabs_d[2 * kind + 1][:, t0:t0 + TH], writes=["tab"])
    return cos, sn


def mla_proj(K, hT, t0, j, w_in, w_q_b, w_kv_b, tabs_d, QN, QP, KN, KP, VT, sub=None):
    S = K.S
    T = TH
    NH = 2
    m0 = K.mark()
    vm = V_MLA + 20 * j
    g_qa = K.vec[:, vm:vm + 12]
    g_kva = K.vec[:, vm + 12:vm + 16]
    g_qn = K.vec[:, vm + 16:vm + 17]
    g_qp = K.vec[:, vm + 17:vm + 18]
    g_kn = K.vec[:, vm + 18:vm + 19]
    g_kp = K.vec[:, vm + 19:vm + 20]
    cos, sn = load_tabs(K, tabs_d, 1, t0)
    if 'tabsonly' in MKX:
        S.barrier()
        K.reset(m0)
        return
    cqg = K.alloc(BF16, 12, T)
    ckvg = K.alloc(BF16, 4, T)
    r_cq = K.alloc(F32, T)
    r_cq2 = K.alloc(F32, T)
    r_kv = K.alloc(F32, T)
    r_kv2 = K.alloc(F32, T)
    r_kvc = K.alloc(F32, 8)
    colt = K.alloc(F32, 32)
    hn = HN(K)
    sqb = [K.alloc(BF16, T) for _ in range(2)]
    stg = [K.alloc(BF16, T) for _ in range(2)]
    wbuf = [K.alloc(BF16, 4608) for _ in range(2)]
    win = [w[:, 0:4096].rearrange("p (c n) -> p c n", n=128) for w in wbuf]
    wq = [w.rearrange("p (c n) -> p c n", n=384) for w in wbuf]
    Wv = w_in.rearrange("(c p) n -> p c n", p=128)
    def finstat(stat, r, r2, dim, key):
        for h in range(NH):
            S.add("act", lambda e, h=h: e.activation(out=r[:, h * 512:(h + 1) * 512], in_=stat[h][1][:, :], func=AF.Ln,
                                                       scale=1.0 / dim, bias=EPS), reads=[PS(stat[h])], writes=[key])
        S.add("act", lambda e: e.activation(out=r, in_=r, func=AF.Exp, scale=-0.5), reads=[key], writes=[key])
        S.add("dve", lambda e: e.tensor_tensor(out=r2, in0=r, in1=r, op=ALU.mult), reads=[key], writes=[key])
    order = [("kv", c) for c in range(4)] + ([("pe", 0)] if 'nope' not in MKX else []) + [("cq", c) for c in range(12)]
    st_kv = K.reserve(2)
    st_col = K.reserve(1)[0]
    st_cq = None
    K.set_aux(2)
    for (kind, c) in order:
        s = K.nxt("win", 2)
        if kind == "pe":
            dma(K, "pool", "win%d" % s, win[s], Wv[:, :, 1984:2112], writes=[("win", s)])
        else:
            col = (1536 + c * 128) if kind == "kv" else c * 128
            dma(K, "pool", "win%d" % s, win[s], Wv[:, :, col:col + 128], writes=[("win", s)])
        if kind == "cq" and st_cq is None:
            finstat(st_kv, r_kv, r_kv2, 512, "rkv")
            S.add("act", lambda e: e.activation(out=colt, in_=st_col[1][:, 0:32], func=AF.Identity),
                  reads=[PS(st_col)], writes=["colt"])
            S.add("dve", lambda e: e.tensor_tensor(out=colt[:, 0:16], in0=colt[:, 0:16], in1=colt[:, 16:32], op=ALU.add),
                  reads=["colt"], writes=["colt"])
            S.add("dve", lambda e: e.tensor_tensor(out=colt[:, 0:8], in0=colt[:, 0:8], in1=colt[:, 8:16], op=ALU.add),
                  reads=["colt"], writes=["colt"])
            S.add("act", lambda e: e.activation(out=r_kvc, in_=colt[:, 0:8], func=AF.Sqrt, scale=1.0 / 512, bias=EPS),
                  reads=["colt"], writes=["rkvc"])
            S.add("dve", lambda e: e.reciprocal(out=r_kvc, in_=r_kvc), reads=["rkvc"], writes=["rkvc"])
            K.release(st_kv + [st_col])
            st_cq = K.reserve(2)
        pa = [K.psum() for _ in range(NH)]

        def mm(e, s=s, pa=pa, kind=kind):
            ins = None
            for k in range(DC):
                for h in range(NH):
                    if kind == "pe":
                        ins = e.matmul(pa[h][1][0:64, :], lhsT=win[s][:, k, 64:128], rhs=hT[:, k, h * 512:(h + 1) * 512],
                                       start=(k == 0), stop=(k == DC - 1))
                    else:
                        ins = e.matmul(pa[h][1][:, :], lhsT=win[s][:, k, :], rhs=hT[:, k, h * 512:(h + 1) * 512],
                                       start=(k == 0), stop=(k == DC - 1))
            return ins
        S.add("pe", mm, reads=[("win", s), "hT"], writes=[PS(p) for p in pa])
        if kind == "pe":
            for h in range(NH):
                q = K.nxt("stg", 2)
                sl_ = slice(h * 512, (h + 1) * 512)
                hn.run(pa[h], 64, g_kp[0:64], K.bd[0:64, 0:64], stg[q][0:64, sl_], ("stg", q, h),
                       rope=(cos[0:64, sl_], sn[0:64, sl_], K.perm_m[0:64, 0:64], "tab"), P=64)
                dma(K, "sp", "stgo%d" % q, KP[:, t0 + h * 512:t0 + (h + 1) * 512], stg[q][0:64, sl_],
                    reads=[("stg", q, h)], writes=[("stg", q, h)])
            continue
        q = K.nxt("sqb", 2)
        dst = ckvg if kind == "kv" else cqg
        gv = g_kva if kind == "kv" else g_qa
        stat = st_kv if kind == "kv" else st_cq
        nck = 4 if kind == "kv" else 12
        for h in range(NH):
            S.add("act", lambda e, q=q, h=h, pa=pa: e.activation(out=sqb[q][:, h * 512:(h + 1) * 512], in_=pa[h][1][:, :],
                                                                  func=AF.Square),
                  reads=[PS(pa[h])], writes=[("sqb", q, h)])
            S.add("act", lambda e, h=h, pa=pa, dst=dst, gv=gv, c=c: e.activation(
                out=dst[:, c, h * 512:(h + 1) * 512], in_=pa[h][1][:, :], func=AF.Identity, scale=gv[:, c:c + 1], bias=0.0),
                reads=[PS(pa[h]), "vec"], writes=[(kind, c, h)])

        def mmst(e, q=q, c=c, stat=stat, nck=nck, kind=kind):
            ins = None
            for h in range(NH):
                ins = e.matmul(stat[h][1][:, :], lhsT=K.ones_b, rhs=sqb[q][:, h * 512:(h + 1) * 512],
                               start=(c == 0), stop=(c == nck - 1))
            if kind == "kv" and 'nocol' not in MKX:
                for tt in range(8):
                    ins = e.matmul(st_col[1][:, c * 8 + tt:c * 8 + tt + 1], lhsT=sqb[q][:, tt * 128:(tt + 1) * 128],
                                   rhs=K.ones_b[:, 0:1], start=True, stop=True)
            return ins
        S.add("pe", mmst, reads=[("sqb", q, 0), ("sqb", q, 1)], writes=[PS(p) for p in stat] + ([PS(st_col)] if kind == "kv" else []))

    finstat(st_cq, r_cq, r_cq2, 1536, "rcq")
    K.release(st_cq)
    K.clear_aux()
    K.set_aux(4)

    def fin():
        K.clear_aux()
        S.barrier()
        K.reset(m0)
    if sub == "pa":
        return fin()
    cq_keys = [("cq", c, h) for c in range(12) for h in range(2)]
    kv_keys = [("kv", c, h) for c in range(4) for h in range(2)]
    Wq = w_q_b.rearrange("(c p) n -> p c n", p=128)
    pendq = [None]
    for hp in range(16):
        s = K.nxt("win", 2)
        dma(K, "pool", "win%d" % s, wq[s], Wq[:, :, hp * 384:(hp + 1) * 384], writes=[("win", s)])
        w3 = wq[s]
        for part in range(4):
            ispe = part % 2 == 1
            hd = 2 * hp + part // 2
            c0 = (part // 2) * 192 + (128 if ispe else 0)
            M = 64 if ispe else 128
            pa = [K.psum() for _ in range(NH)]

            def mm(e, s=s, pa=pa, c0=c0, M=M, w3=w3):
                ins = None
                for k in range(12):
                    for h in range(NH):
                        ins = e.matmul(pa[h][1][0:M, :], lhsT=w3[:, k, c0:c0 + M], rhs=cqg[:, k, h * 512:(h + 1) * 512],
                                       start=(k == 0), stop=(k == 11))
                return ins
            S.add("pe", mm, reads=[("win", s)] + cq_keys, writes=[PS(p) for p in pa])

            def epi(pa=pa, ispe=ispe, hd=hd):
                q = K.nxt("stg", 2)
                items = []
                for h in range(NH):
                    sl_ = slice(h * 512, (h + 1) * 512)
                    if not ispe:
                        items.append(dict(pb=pa[h], dim=128, gain=g_qn, ones=K.ones_b, out=stg[q][:, sl_], key=("stg", q, h),
                                          pre=(r_cq[:, sl_], r_cq2[:, sl_], "rcq"), rope=None, P=128))
                    else:
                        items.append(dict(pb=pa[h], dim=64, gain=g_qp[0:64], ones=K.bd[0:64, 0:64], out=stg[q][0:64, sl_],
                                          key=("stg", q, h), pre=(r_cq[0:64, sl_], r_cq2[0:64, sl_], "rcq"),
                                          rope=(cos[0:64, sl_], sn[0:64, sl_], K.perm_m[0:64, 0:64], "tab"), P=64))
                hn.run2(items)
                if not ispe:
                    dma(K, "sp", "stgo%d" % q, QN[hd][:, t0:t0 + T], stg[q], reads=[("stg", q, 0), ("stg", q, 1)],
                        writes=[("stg", q, 0), ("stg", q, 1)])
                else:
                    dma(K, "sp", "stgo%d" % q, QP[hd][:, t0:t0 + T], stg[q][0:64, :], reads=[("stg", q, 0), ("stg", q, 1)],
                        writes=[("stg", q, 0), ("stg", q, 1)])
            if pendq[0] is not None:
                pendq[0]()
            pendq[0] = epi
    pendq[0]()
    if sub == "pq":
        return fin()
    wk = [K.alloc(BF16, 4, 128) for _ in range(2)]
    Wk = w_kv_b.rearrange("(c p) n -> p c n", p=128)
    pendk = [None]
    for hd in range(32):
        s = K.nxt("wk", 2)
        dma(K, "pool", "wk%d" % s, wk[s], Wk[:, :, hd * 256:hd * 256 + 128], writes=[("wk", s)])
        pa = [K.psum() for _ in range(NH)]

        def mm(e, s=s, pa=pa):
            ins = None
            for k in range(4):
                for h in range(NH):
                    ins = e.matmul(pa[h][1][:, :], lhsT=wk[s][:, k, :], rhs=ckvg[:, k, h * 512:(h + 1) * 512],
                                   start=(k == 0), stop=(k == 3))
            return ins
        S.add("pe", mm, reads=[("wk", s)] + kv_keys, writes=[PS(p) for p in pa])

        def epi(pa=pa, hd=hd):
            q = K.nxt("stg", 2)
            items = []
            for h in range(NH):
                sl_ = slice(h * 512, (h + 1) * 512)
                items.append(dict(pb=pa[h], dim=128, gain=g_kn, ones=K.ones_b, out=stg[q][:, sl_], key=("stg", q, h),
                                  pre=(r_kv[:, sl_], r_kv2[:, sl_], "rkv"), rope=None, P=128))
            hn.run2(items)
            dma(K, "sp", "stgo%d" % q, KN[hd][:, t0:t0 + T], stg[q], reads=[("stg", q, 0), ("stg", q, 1)],
                writes=[("stg", q, 0), ("stg", q, 1)])
        if pendk[0] is not None:
            pendk[0]()
        pendk[0] = epi
    pendk[0]()
    if sub == "pk":
        return fin()
    wv = [K.alloc(BF16, 4, 512) for _ in range(2)]
    vs = [K.alloc(BF16, 512) for _ in range(2)]
    Wv4 = w_kv_b.rearrange("(c p) (h two d) -> p c h two d", p=128, two=2, d=128)
    for hg in range(8):
        s = K.nxt("wv", 2)
        dma_multi(K, "pool", "wv%d" % s, [(wv[s][:, c, :].rearrange("p (h d) -> p h d", d=128), Wv4[:, c, hg * 4:(hg + 1) * 4, 1, :])
                                          for c in range(4)], writes=[("wv", s)])
        for tt in range(8):
            pb = K.psum()

            def mm(e, s=s, pb=pb, tt=tt):
                ins = None
                for k in range(4):
                    ins = e.matmul(pb[1][:, :], lhsT=ckvg[:, k, tt * 128:(tt + 1) * 128], rhs=wv[s][:, k, :],
                                   start=(k == 0), stop=(k == 3))
                return ins
            S.add("pe", mm, reads=[("wv", s)] + kv_keys, writes=[PS(pb)])
            q = K.nxt("vs", 2)
            S.add("act", lambda e, q=q, pb=pb, tt=tt: e.activation(out=vs[q], in_=pb[1][:, :], func=AF.Identity,
                                                                   scale=r_kvc[:, tt:tt + 1], bias=0.0),
                  reads=[PS(pb), "rkvc"], writes=[("vs", q)])
            dma(K, "sp", "vso%d" % q, VT[t0 + tt * 128:t0 + (tt + 1) * 128, hg * 512:(hg + 1) * 512], vs[q],
                reads=[("vs", q)], writes=[("vs", q)])
    K.clear_aux()
    S.barrier()
    K.reset(m0)


def mla_attn(K, QN, QP, KN, KP, VT, OT):
    S = K.S
    m0 = K.mark()
    scale = 192.0 ** -0.5
    kp = K.alloc(BF16, SEQ)
    dma(K, "sp", "kp", kp[0:64, :], KP, writes=["kp"])
    qn = [K.alloc(BF16, SEQ) for _ in range(2)]
    kn = [K.alloc(BF16, SEQ) for _ in range(2)]
    qp = [K.alloc(BF16, SEQ) for _ in range(2)]
    vv = [K.alloc(BF16, 16, 128) for _ in range(2)]
    ob = [K.alloc(BF16, SEQ) for _ in range(2)]
    E = [K.alloc(BF16, 512) for _ in range(3)]
    rd = [K.alloc(F32, 512) for _ in range(2)]
    VTv = VT.rearrange("(kt p) c -> p kt c", p=128)
    for hd in range(32):
        s = hd % 2
        ps_ = s
        pbase = 0
        dma(K, "sp", "aqn%d" % s, qn[s], QN[hd], writes=[("qn", s)])
        dma(K, "sp", "akn%d" % s, kn[s], KN[hd], writes=[("kn", s)])
        dma(K, "sp", "aqp%d" % s, qp[s][0:64, :], QP[hd], writes=[("qp", s)])
        dma(K, "sp", "avv%d" % s, vv[s], VTv[:, :, hd * 128:(hd + 1) * 128], writes=[("vv", s)])
        for qb in range(4):
            acc = K.reserve(2)
            nkt = 4 * qb + 4
            pend = None
            for kt in range(nkt):
                r = kt - 4 * qb
                q0 = 128 * r if r > 0 else 0
                sb = K.psum()

                def mms(e, s=s, ps_=ps_, sb=sb, kt=kt, qb=qb, q0=q0, pbase=pbase):
                    e.matmul(sb[1][:, q0:512], lhsT=kn[s][:, kt * 128:(kt + 1) * 128], rhs=qn[s][:, qb * 512 + q0:(qb + 1) * 512],
                             start=True, stop=False)
                    return e.matmul(sb[1][:, q0:512], lhsT=kp[pbase:pbase + 64, kt * 128:(kt + 1) * 128],
                                    rhs=qp[ps_][pbase:pbase + 64, qb * 512 + q0:(qb + 1) * 512], start=False, stop=True)
                S.add("pe", mms, reads=[("qn", s), ("kn", s), ("qp", ps_), "kp"], writes=[PS(sb)])
                ei = K.nxt("E", 3)
                S.add("act", lambda e, ei=ei, sb=sb, q0=q0: e.activation(out=E[ei][:, q0:512], in_=sb[1][:, q0:512],
                                                                         func=AF.Exp, scale=scale),
                      reads=[PS(sb)], writes=[("E", ei)])
                if r >= 0:
                    S.add("dve", lambda e, ei=ei, q0=q0: e.tensor_tensor(out=E[ei][:, q0:q0 + 128], in0=E[ei][:, q0:q0 + 128],
                                                                         in1=K.tri, op=ALU.mult),
                          reads=[("E", ei)], writes=[("E", ei)])

                def mmo(e, s=s, ei=ei, kt=kt, q0=q0, acc=acc, nkt=nkt):
                    e.matmul(acc[0][1][:, q0:512], lhsT=vv[s][:, kt, :], rhs=E[ei][:, q0:512],
                             start=(kt == 0), stop=(kt == nkt - 1))
                    return e.matmul(acc[1][1][:, q0:512], lhsT=K.ones_b, rhs=E[ei][:, q0:512],
                                    start=(kt == 0), stop=(kt == nkt - 1))
                if pend is not None:
                    pend()
                pend = (lambda mmo=mmo, ei=ei, s=s, acc=acc: S.add("pe", mmo, reads=[("E", ei), ("vv", s)],
                                                                    writes=[PS(acc[0]), PS(acc[1])]))
            pend()
            ri = K.nxt("rd", 2)
            S.add("act", lambda e, ri=ri, acc=acc: e.activation(out=rd[ri], in_=acc[1][1][:, :], func=AF.Ln),
                  reads=[PS(acc[1])], writes=[("rd", ri)])
            S.add("act", lambda e, ri=ri: e.activation(out=rd[ri], in_=rd[ri], func=AF.Exp, scale=-1.0),
                  reads=[("rd", ri)], writes=[("rd", ri)])
            S.add("dve", lambda e, ri=ri, acc=acc, s=s, qb=qb: e.tensor_tensor(out=ob[s][:, qb * 512:(qb + 1) * 512],
                                                                                  in0=acc[0][1][:, :], in1=rd[ri], op=ALU.mult),
                  reads=[PS(acc[0]), ("rd", ri)], writes=[("ob", s, qb)])
            K.release(acc)
        dma(K, "sp", "aob%d" % s, OT[hd * 128:(hd + 1) * 128, :], ob[s], reads=[("ob", s, qb) for qb in range(4)],
            writes=[("ob", s, qb) for qb in range(4)])
    S.barrier()
    K.reset(m0)


DIL_R = (1, 4, 16)


def dil_proj(K, hT, t0, j, w_qkv, tabs_d, Qd, Kd, Vd):
    S = K.S
    T = TH
    NH = 2
    m0 = K.mark()
    vd = V_DIL + 6 * j
    cos, sn = load_tabs(K, tabs_d, 0, t0)
    hn = HN(K)
    K.set_aux(4)
    stg = [K.alloc(BF16, T) for _ in range(2)]
    wt = [K.alloc(BF16, DC, 256) for _ in range(2)]
    Wv = w_qkv.rearrange("(c p) n -> p c n", p=128)
    pend = [None]
    for sgi in range(2):
        for g in range(3):
            gain = K.vec[:, vd + 3 * sgi + g:vd + 3 * sgi + g + 1]
            for h2 in range(8):
                s = K.nxt("dwt", 2)
                col = ((sgi * 3 + g) * 16 + 2 * h2) * 128
                dma(K, "pool", "dwt%d" % s, wt[s], Wv[:, :, col:col + 256], writes=[("wt", s)])
                for hh in range(2):
                    pa = [K.psum() for _ in range(NH)]

                    def mm(e, s=s, pa=pa, hh=hh):
                        ins = None
                        for k in range(DC):
                            for h in range(NH):
                                ins = e.matmul(pa[h][1][:, :], lhsT=wt[s][:, k, hh * 128:(hh + 1) * 128],
                                               rhs=hT[:, k, h * 512:(h + 1) * 512], start=(k == 0), stop=(k == DC - 1))
                        return ins
                    S.add("pe", mm, reads=[("wt", s), "hT"], writes=[PS(p) for p in pa])

                    def epi(pa=pa, gain=gain, sgi=sgi, g=g, h2=h2, hh=hh):
                        q = K.nxt("stg", 2)
                        items = []
                        for h in range(NH):
                            sl_ = slice(h * 512, (h + 1) * 512)
                            items.append(dict(pb=pa[h], dim=128, gain=gain, ones=K.ones_b, out=stg[q][:, sl_], key=("stg", q, h),
                                              pre=None, rope=(cos[:, sl_], sn[:, sl_], K.perm_d, "tab"), P=128))
                        hn.run2(items)
                        dst = (Qd if sgi == 0 else Kd)[g * 16 + 2 * h2 + hh]
                        dma(K, "sp", "stgo%d" % q, dst[:, t0:t0 + T], stg[q], reads=[("stg", q, 0), ("stg", q, 1)],
                            writes=[("stg", q, 0), ("stg", q, 1)])
                    if pend[0] is not None:
                        pend[0]()
                    pend[0] = epi
    pend[0]()
    K.clear_aux()
    S.barrier()
    K.reset(m0)
    wv = [K.alloc(BF16, DC, 512) for _ in range(2)]
    vs = [K.alloc(BF16, 512) for _ in range(2)]
    for cg in range(12):
        s = K.nxt("dwv", 2)
        col = 2 * 3 * 16 * 128 + cg * 512
        dma(K, "pool", "dwv%d" % s, wv[s], Wv[:, :, col:col + 512], writes=[("wv", s)])
        for tt in range(8):
            pb = K.psum()

            def mm(e, s=s, pb=pb, tt=tt):
                ins = None
                for k in range(DC):
                    ins = e.matmul(pb[1][:, :], lhsT=hT[:, k, tt * 128:(tt + 1) * 128], rhs=wv[s][:, k, :],
                                   start=(k == 0), stop=(k == DC - 1))
                return ins
            S.add("pe", mm, reads=[("wv", s), "hT"], writes=[PS(pb)])
            q = K.nxt("vs", 2)
            S.add("act", lambda e, q=q, pb=pb: e.activation(out=vs[q], in_=pb[1][:, :], func=AF.Identity),
                  reads=[PS(pb)], writes=[("vs", q)])
            dma(K, "sp", "vso%d" % q, Vd[t0 + tt * 128:t0 + (tt + 1) * 128, cg * 512:(cg + 1) * 512], vs[q],
                reads=[("vs", q)], writes=[("vs", q)])
    S.barrier()
    K.reset(m0)


def dil_attn(K, Qd, Kd, Vd, OT):
    S = K.S
    m0 = K.mark()
    scale = 128.0 ** -0.5
    accO = K.alloc(F32, SEQ)
    accD = K.alloc(F32, SEQ)
    qd = [K.alloc(BF16, SEQ) for _ in range(2)]
    kd = [K.alloc(BF16, SEQ) for _ in range(2)]
    vv = [K.alloc(BF16, 16, 128) for _ in range(2)]
    E = [K.alloc(BF16, 256) for _ in range(3)]
    ob = [K.alloc(BF16, SEQ) for _ in range(2)]
    for hh in range(16):
        for g in range(3):
            r = DIL_R[g]
            L = SEQ // r
            M = L // 128
            s = K.nxt("dq", 2)
            idx = g * 16 + hh
            dma(K, "sp", "dqd%d" % s, qd[s], Qd[idx], writes=[("qd", s)])
            dma(K, "sp", "dkd%d" % s, kd[s], Kd[idx], writes=[("kd", s)])
            vsrc = Vd[:, idx * 128:(idx + 1) * 128].rearrange("(m kj r) d -> kj r m d", kj=128, r=r)
            for rho in range(r):
                dma(K, "sp", "dvv%d" % s, vv[s][:, rho * M:(rho + 1) * M, :], vsrc[:, rho], writes=[("vv", s, rho)])
            qv = qd[s].rearrange("p (l r) -> p r l", r=r)
            kv = kd[s].rearrange("p (l r) -> p r l", r=r)
            aO = accO.rearrange("p (l r) -> p r l", r=r)
            aD = accD.rearrange("p (l r) -> p r l", r=r)
            blocks = [(rho, n) for rho in range(r) for n in range(M)]
            for c4 in range(4):
                acc = K.reserve(2)
                pend = None
                for bi in range(4):
                    rho, n = blocks[c4 * 4 + bi]
                    sb = K.psum()
                    c0 = 0 if n > 0 else 128

                    def mms(e, s=s, sb=sb, rho=rho, n=n, qv=qv, kv=kv):
                        if n > 0:
                            e.matmul(sb[1][:, 0:128], lhsT=kv[:, rho, (n - 1) * 128:n * 128], rhs=qv[:, rho, n * 128:(n + 1) * 128],
                                     start=True, stop=True)
                        return e.matmul(sb[1][:, 128:256], lhsT=kv[:, rho, n * 128:(n + 1) * 128],
                                        rhs=qv[:, rho, n * 128:(n + 1) * 128], start=True, stop=True)
                    S.add("pe", mms, reads=[("qd", s), ("kd", s)], writes=[PS(sb)])
                    ei = K.nxt("dE", 3)
                    S.add("act", lambda e, ei=ei, sb=sb, c0=c0: e.activation(out=E[ei][:, c0:256], in_=sb[1][:, c0:256],
                                                                             func=AF.Exp, scale=scale),
                          reads=[PS(sb)], writes=[("E", ei)])
                    S.add("dve", lambda e, ei=ei, c0=c0: e.tensor_tensor(out=E[ei][:, c0:256], in0=E[ei][:, c0:256],
                                                                         in1=K.dmask[:, c0:256], op=ALU.mult),
                          reads=[("E", ei)], writes=[("E", ei)])

                    def mmo(e, s=s, ei=ei, rho=rho, n=n, bi=bi, acc=acc, M=M):
                        cs = slice(bi * 128, (bi + 1) * 128)
                        if n > 0:
                            e.matmul(acc[0][1][:, cs], lhsT=vv[s][:, rho * M + n - 1, :], rhs=E[ei][:, 0:128], start=True, stop=False)
                        e.matmul(acc[0][1][:, cs], lhsT=vv[s][:, rho * M + n, :], rhs=E[ei][:, 128:256], start=(n == 0), stop=True)
                        if n > 0:
                            e.matmul(acc[1][1][:, cs], lhsT=K.ones_b, rhs=E[ei][:, 0:128], start=True, stop=False)
                        return e.matmul(acc[1][1][:, cs], lhsT=K.ones_b, rhs=E[ei][:, 128:256], start=(n == 0), stop=True)
                    if pend is not None:
                        pend()
                    pend = (lambda mmo=mmo, ei=ei, s=s, acc=acc, r=r: S.add(
                        "pe", mmo, reads=[("E", ei)] + [("vv", s, rr) for rr in range(r)], writes=[PS(acc[0]), PS(acc[1])]))
                pend()
                if r == 1:
                    dO = aO[:, 0, c4 * 512:(c4 + 1) * 512]
                    dD = aD[:, 0, c4 * 512:(c4 + 1) * 512]
                    sO = acc[0][1][:, :]
                    sD = acc[1][1][:, :]
                elif r == 4:
                    dO = aO[:, c4, :]
                    dD = aD[:, c4, :]
                    sO = acc[0][1][:, :]
                    sD = acc[1][1][:, :]
                else:
                    dO = aO[:, c4 * 4:(c4 + 1) * 4, :]
                    dD = aD[:, c4 * 4:(c4 + 1) * 4, :]
                    sO = acc[0][1][:, :].rearrange("p (a b) -> p a b", b=128)
                    sD = acc[1][1][:, :].rearrange("p (a b) -> p a b", b=128)
                if g == 0:
                    S.add("dve", lambda e, dO=dO, sO=sO: e.tensor_copy(out=dO, in_=sO), reads=[PS(acc[0])], writes=["accO"])
                    S.add("dve", lambda e, dD=dD, sD=sD: e.tensor_copy(out=dD, in_=sD), reads=[PS(acc[1])], writes=["accD"])
                else:
                    S.add("dve", lambda e, dO=dO, sO=sO: e.tensor_tensor(out=dO, in0=dO, in1=sO, op=ALU.add),
                          reads=[PS(acc[0]), "accO"], writes=["accO"])
                    S.add("dve", lambda e, dD=dD, sD=sD: e.tensor_tensor(out=dD, in0=dD, in1=sD, op=ALU.add),
                          reads=[PS(acc[1]), "accD"], writes=["accD"])
                K.release(acc)
        o = hh % 2
        S.add("act", lambda e: e.activation(out=accD, in_=accD, func=AF.Ln), reads=["accD"], writes=["accD"])
        S.add("act", lambda e: e.activation(out=accD, in_=accD, func=AF.Exp, scale=-1.0), reads=["accD"], writes=["accD"])
        S.add("dve", lambda e, o=o: e.tensor_tensor(out=ob[o], in0=accO, in1=accD, op=ALU.mult),
              reads=["accO", "accD"], writes=[("ob", o)])
        dma(K, "sp", "dob%d" % o, OT[hh * 128:(hh + 1) * 128, :], ob[o], reads=[("ob", o)], writes=[("ob", o)])
    S.barrier()
    K.reset(m0)


def build(n_layers=DEPTH, stop=None, declared=None):
    nc = bass.Bass("TRN2", target_bir_lowering=False)
    cache = {}

    def inp(name, shape, dt=F32):
        if name not in cache:
            cache[name] = nc.dram_tensor(name, list(shape), dt, kind="ExternalInput").ap()
            if declared is not None:
                declared.append(name)
        return cache[name]
    xT = inp("xT", [D, SEQ])
    posb = inp("posb", [128, SEQ], I32)
    vecs = inp("vecs", [128, NV])
    cmat = inp("cmat", [128, NCM])
    w_cond = inp("w_cond", [D, 512])
    w_mod = lambda i: inp("w_mod%d" % i, [512, 6 * D])
    mla_w_in = lambda j: inp("mla_w_in%d" % j, [D, 2112])
    mla_w_q_b = lambda j: inp("mla_w_q_b%d" % j, [1536, 6144])
    mla_w_kv_b = lambda j: inp("mla_w_kv_b%d" % j, [512, 8192])
    mla_w_o = lambda j: inp("mla_w_o%d" % j, [D, D])
    dil_w_qkv = lambda j: inp("dil_w_qkv%d" % j, [D, 18432])
    dil_w_o = lambda j: inp("dil_w_o%d" % j, [2048, D])
    ffn_wg = lambda i: inp("ffn_wg%d" % i, [D, HID])
    ffn_wu = lambda i: inp("ffn_wu%d" % i, [D, HID])
    ffn_wd = lambda i: inp("ffn_wd%d" % i, [HID, D])
    y = nc.dram_tensor("y", [D, SEQ], F32, kind="ExternalOutput").ap()

    def scr(name, shape, dt=BF16):
        return nc.dram_tensor(name, list(shape), dt).ap()
    tabs = scr("tabs", [4, 128, SEQ], F32)
    QN = scr("QN", [48, 128, SEQ])
    KN = scr("KN", [48, 128, SEQ])
    QP = scr("QP", [32, 64, SEQ])
    KP = scr("KP", [64, SEQ])
    VT = scr("VT", [SEQ, 6144])
    OT = scr("OT", [D, SEQ])

    with ExitStack() as st:
        K = Ctx(nc, st)
        S = K.S

        def body():
            setup(K, vecs, cmat)
            dma(K, "sp", "xcp", y, xT)
            rope_tables(K, posb, tabs)
            cond_embed(K, w_cond)
            if stop == "setup":
                return
            hT = K.alloc(BF16, DC, TH)
            mv = K.modv
            A_m, B_m, gt_m = mv[:, 0:DC], mv[:, DC:2 * DC], mv[:, 2 * DC:3 * DC]
            A_f, B_f, gt_f = mv[:, 3 * DC:4 * DC], mv[:, 4 * DC:5 * DC], mv[:, 5 * DC:6 * DC]
            VTm = VT[:, 0:4096]
            for li in range(n_layers):
                j = li // 2
                mod_layer(K, w_mod(li), li)
                if stop == "mod":
                    return
                if li % 2 == 0:
                    for th in range(2):
                        norm_phase(K, y, th * TH, A_m, B_m, hT)
                        if stop == "norm":
                            return
                        mla_proj(K, hT, th * TH, j, mla_w_in(j), mla_w_q_b(j), mla_w_kv_b(j), tabs, QN, QP, KN, KP, VTm, sub=stop)
                        if stop in ('pa', 'pq', 'pk'):
                            return
                    if stop == "proj":
                        return
                    mla_attn(K, QN, QP, KN, KP, VTm, OT)
                    if stop == "attn":
                        return
                    for th in range(2):
                        dma(K, "sp", "oth", hT, OT.rearrange("(c p) t -> p c t", p=128)[:, :, th * TH:(th + 1) * TH], writes=["hT"])
                        proj_residual(K, hT, DC, mla_w_o(j), gt_m, y, th * TH, "mo")
                else:
                    for th in range(2):
                        norm_phase(K, y, th * TH, A_m, B_m, hT)
                        dil_proj(K, hT, th * TH, j, dil_w_qkv(j), tabs, QN, KN, VT)
                    if stop == "dproj":
                        return
                    dil_attn(K, QN, KN, VT, OT)
                    if stop == "dattn":
                        return
                    for th in range(2):
                        dma(K, "sp", "oth", hT[:, 0:16, :], OT[0:2048, :].rearrange("(c p) t -> p c t", p=128)[:, :, th * TH:(th + 1) * TH],
                            writes=["hT"])
                        proj_residual(K, hT, 16, dil_w_o(j), gt_m, y, th * TH, "do")
                if stop == "mix" and li == n_layers - 1:
                    return
                for th in range(2):
                    norm_phase(K, y, th * TH, A_f, B_f, hT)
                    ffn_phase(K, hT, y, th * TH, ffn_wg(li), ffn_wu(li), ffn_wd(li), gt_f)
        body()
        S.barrier()
        S.emit(nc, st)
    return nc


def _arr(v):
    v = np.asarray(v, np.float32)
    return np.ascontiguousarray(v.reshape(-1, 128).T)


def make_consts():
    cm = np.zeros((128, NCM), np.float32)
    for m in range(128):
        cm[(m + 64) % 128, C_PERM_D + m] = 1.0
        blk = (m // 64) * 64
        cm[blk + ((m % 64) + 32) % 64, C_PERM_M + m] = 1.0
    cm[0:64, C_BD:C_BD + 64] = 1.0
    cm[64:128, C_BD + 64:C_BD + 128] = 1.0
    k = np.arange(128)[:, None]
    q = np.arange(128)[None, :]
    cm[:, C_TRI:C_TRI + 128] = (k <= q)
    cm[:, C_DMASK:C_DMASK + 128] = (k >= q)
    cm[:, C_DMASK + 128:C_DMASK + 256] = (k <= q)
    p = np.arange(128)
    invf_d = np.exp((p % 64).astype(np.float32) * np.float32(-2.0 * math.log(10000.0) / 128)).astype(np.float32)
    invf_m = np.exp((p % 32).astype(np.float32) * np.float32(-2.0 * math.log(10000.0) / 64)).astype(np.float32)
    sgn_d = np.where(p < 64, -1.0, 1.0).astype(np.float32)
    sgn_m = np.where((p % 64) < 32, -1.0, 1.0).astype(np.float32)
    return cm, np.stack([invf_d, invf_m, sgn_d, sgn_m], axis=1)


def make_in_map(b, I, cm, consts):
    vec = np.zeros((128, NV), np.float32)
    vec[:, V_C:V_C + 32] = _arr(I["c"][b])
    vec[:, V_BCOND:V_BCOND + 4] = _arr(I["b_cond"])
    for i in range(DEPTH):
        o = V_LAYER + 256 * i
        vec[:, o:o + 192] = _arr(I["b_mod"][i])
        vec[:, o + 192:o + 224] = _arr(I["g_mix_norm"][i])
        vec[:, o + 224:o + 256] = _arr(I["g_ffn_norm"][i])
    for j in range(2):
        o = V_MLA + 20 * j
        vec[:, o:o + 12] = _arr(I["mla_g_q_a"][j])
        vec[:, o + 12:o + 16] = _arr(I["mla_g_kv_a"][j])
        vec[:, o + 16] = I["mla_g_q_nope"][j]
        vec[:, o + 17] = np.tile(I["mla_g_q_pe"][j], 2)
        vec[:, o + 18] = I["mla_g_k_nope"][j]
        vec[:, o + 19] = np.tile(I["mla_g_k_pe"][j], 2)
        o = V_DIL + 6 * j
        vec[:, o:o + 3] = np.asarray(I["dil_g_q"][j]).T
        vec[:, o + 3:o + 6] = np.asarray(I["dil_g_k"][j]).T
    vec[:, V_CONST:V_CONST + 4] = consts
    m = {
        "xT": np.ascontiguousarray(np.asarray(I["x"][b]).T),
        "posb": np.ascontiguousarray(np.broadcast_to(np.asarray(I["positions"][b], np.int32)[None, :], (128, SEQ))),
        "vecs": vec, "cmat": cm, "w_cond": np.asarray(I["w_cond"]),
    }
    for i in range(DEPTH):
        m["w_mod%d" % i] = np.asarray(I["w_mod"][i])
        m["ffn_wg%d" % i] = np.asarray(I["ffn_w_gate"][i])
        m["ffn_wu%d" % i] = np.asarray(I["ffn_w_up"][i])
        m["ffn_wd%d" % i] = np.asarray(I["ffn_w_down"][i])
    for j in range(2):
        m["mla_w_in%d" % j] = np.asarray(I["mla_w_in"][j])
        m["mla_w_q_b%d" % j] = np.asarray(I["mla_w_q_b"][j])
        m["mla_w_kv_b%d" % j] = np.asarray(I["mla_w_kv_b"][j])
        m["mla_w_o%d" % j] = np.asarray(I["mla_w_o"][j])
        m["dil_w_qkv%d" % j] = np.asarray(I["dil_w_qkv"][j])
        m["dil_w_o%d" % j] = np.asarray(I["dil_w_o"][j])
    return m


def kernel(**inputs):
    I = inputs
    cm, consts = make_consts()
    nc = build(DEPTH)
    B = np.asarray(I["x"]).shape[0]
    in_maps = [make_in_map(b, I, cm, consts) for b in range(B)]
    res = run_bass_kernel_spmd(nc, in_maps, core_ids=list(range(B)))
    out = np.stack([np.ascontiguousarray(res.results[b]["y"].T) for b in range(B)], axis=0)
    return out.astype(np.float32)
```
